# Optimizing a Trainium2 kernel written in Bass

```python
import math
import jax, jax.numpy as jnp
from jax import lax
import numpy as np

D_MODEL = 1024
BATCH = 16
SEQ = 2048
DEPTH = 2
DEC_BATCH = 128
DEC_SEQ = 8
PAST_LEN = 16384
PAGE_SIZE = 128

N_A_LAYERS = DEPTH // 2
N_B_LAYERS = DEPTH - N_A_LAYERS
A_HEADS = 8
A_DK = 128
A_DV = 128
A_KD = A_HEADS * A_DK
A_VD = A_HEADS * A_DV
A_CONV_CH = 2 * A_KD + A_VD
A_IN = A_CONV_CH + A_VD + 2 * A_HEADS
CONV_W = 4
CHUNK = 64
B_Q_HEADS = 16
B_KV_HEADS = 4
B_HEAD_DIM = 64
B_GROUP = B_Q_HEADS // B_KV_HEADS
WINDOW = 128
FFN_HIDDEN = -(-8 * D_MODEL // (3 * 256)) * 256
PLE_DIM = 256
EPS = 1e-6
L2_EPS = 1e-6
NEG_INF = -1e30

kernel_name = 'yoco_gated_delta_swa_sink_step'


def rmsnorm(x, g):
    x32 = x.astype(jnp.float32)
    y = x32 * lax.rsqrt(jnp.mean(x32 * x32, axis=-1, keepdims=True) + EPS)
    return (y * g.astype(jnp.float32)).astype(x.dtype)


def l2norm(x):
    x32 = x.astype(jnp.float32)
    return x32 * lax.rsqrt(jnp.sum(x32 * x32, axis=-1, keepdims=True) + L2_EPS)


def chunk_gated_delta(q, k, v, g, beta, s0):
    bsz, seqlen, nh, _ = q.shape
    dv = v.shape[-1]
    c = min(CHUNK, seqlen)
    n = -(-seqlen // c)
    pad = n * c - seqlen

    def prep(t):
        t = jnp.swapaxes(t, 1, 2)
        widths = [(0, 0)] * t.ndim
        widths[2] = (0, pad)
        t = jnp.pad(t, widths)
        return t.reshape(t.shape[:2] + (n, c) + t.shape[3:])

    q, k, v, g, beta = [prep(t) for t in (q, k, v, g, beta)]
    gc = jnp.cumsum(g, axis=-1)
    idx = jnp.arange(c)
    incl = idx[:, None] >= idx[None, :]
    strict = idx[:, None] > idx[None, :]
    diff = gc[..., :, None] - gc[..., None, :]
    decay = jnp.where(incl, jnp.exp(jnp.where(incl, diff, 0.0)), 0.0)
    kb = k * beta[..., None]
    a_mat = jnp.where(strict, jnp.einsum('bhnid,bhnjd->bhnij', kb, k) * decay, 0.0)
    eye = jnp.eye(c, dtype=jnp.float32)
    rhs = jnp.concatenate([v * beta[..., None], kb * jnp.exp(gc)[..., None]], axis=-1)
    sol = lax.linalg.triangular_solve(a_mat + eye, rhs, left_side=True, lower=True, unit_diagonal=True)
    u_base, w_dec = sol[..., :dv], sol[..., dv:]
    qk = jnp.where(incl, jnp.einsum('bhnid,bhnjd->bhnij', q, k) * decay, 0.0)
    q_dec = q * jnp.exp(gc)[..., None]
    k_dec = k * jnp.exp(gc[..., -1:] - gc)[..., None]
    g_last = jnp.exp(gc[..., -1])
    xs = tuple(jnp.moveaxis(t, 2, 0) for t in (u_base, w_dec, qk, q_dec, k_dec, g_last))

    def step(s, inp):
        ub, wd, qkc, qd, kd, gl = inp
        u = ub - jnp.einsum('bhck,bhkv->bhcv', wd, s)
        o = jnp.einsum('bhck,bhkv->bhcv', qd, s) + jnp.einsum('bhij,bhjv->bhiv', qkc, u)
        s = s * gl[..., None, None] + jnp.einsum('bhck,bhcv->bhkv', kd, u)
        return s, o

    s_fin, o = lax.scan(step, s0, xs)
    o = jnp.moveaxis(o, 0, 2).reshape(bsz, nh, n * c, dv)[:, :, :seqlen]
    return jnp.swapaxes(o, 1, 2), s_fin


def gated_delta_mixer(h, s0, conv_buf, norm_g, w_in, conv_w, a_log, dt_bias, out_g, w_out):
    bsz, seqlen, _ = h.shape
    proj = rmsnorm(h, norm_g) @ w_in
    qkv, z, a, b = jnp.split(proj, [A_CONV_CH, A_CONV_CH + A_VD, A_CONV_CH + A_VD + A_HEADS], axis=-1)
    full = jnp.concatenate([conv_buf.astype(qkv.dtype), qkv], axis=1)
    conv = full[:, 0:seqlen] * conv_w[0]
    for w in range(1, CONV_W):
        conv = conv + full[:, w:w + seqlen] * conv_w[w]
    new_buf = full[:, seqlen:]
    qkv = jax.nn.silu(conv)
    q, k, v = jnp.split(qkv, [A_KD, 2 * A_KD], axis=-1)
    q = l2norm(q.reshape(bsz, seqlen, A_HEADS, A_DK)) * (A_DK ** -0.5)
    k = l2norm(k.reshape(bsz, seqlen, A_HEADS, A_DK))
    v = v.reshape(bsz, seqlen, A_HEADS, A_DV).astype(jnp.float32)
    beta = jax.nn.sigmoid(b.astype(jnp.float32))
    g = -jnp.exp(a_log.astype(jnp.float32)) * jax.nn.softplus(a.astype(jnp.float32) + dt_bias.astype(jnp.float32))
    o, s_new = chunk_gated_delta(q, k, v, g, beta, s0.astype(jnp.float32))
    zf = jax.nn.silu(z.reshape(bsz, seqlen, A_HEADS, A_DV).astype(jnp.float32))
    o = o * lax.rsqrt(jnp.mean(o * o, axis=-1, keepdims=True) + EPS) * out_g.astype(jnp.float32) * zf
    out = o.reshape(bsz, seqlen, A_VD).astype(h.dtype) @ w_out
    return out, s_new.astype(s0.dtype), new_buf.astype(conv_buf.dtype)


def sink_attention(q, k, v, mask, sinks):
    s = jnp.einsum('...qhgd,...khd->...hgqk', q, k).astype(jnp.float32)
    s = jnp.where(mask, s, NEG_INF)
    sk = sinks.astype(jnp.float32).reshape(B_KV_HEADS, B_GROUP, 1, 1)
    m = jnp.maximum(jnp.max(s, axis=-1, keepdims=True), sk)
    p = jnp.exp(s - m)
    prob = p / (jnp.sum(p, axis=-1, keepdims=True) + jnp.exp(sk - m))
    return jnp.einsum('...hgqk,...khd->...qhgd', prob.astype(v.dtype), v)


def window_attention_prompt(q, k, v, sinks):
    bsz, seqlen = q.shape[:2]
    nb = seqlen // WINDOW
    qb = q.reshape(bsz, nb, WINDOW, B_KV_HEADS, B_GROUP, B_HEAD_DIM)

    def band(t):
        tb = t.reshape(bsz, nb, WINDOW, B_KV_HEADS, B_HEAD_DIM)
        prev = jnp.pad(tb[:, :-1], ((0, 0), (1, 0), (0, 0), (0, 0), (0, 0)))
        return jnp.concatenate([prev, tb], axis=2)

    qi = jnp.arange(WINDOW)[:, None]
    ki = jnp.arange(2 * WINDOW)[None, :]
    rel = qi + WINDOW - ki
    kpos = (jnp.arange(nb)[:, None, None] - 1) * WINDOW + ki
    mask = (rel >= 0) & (rel < WINDOW) & (kpos >= 0)
    o = sink_attention(qb, band(k), band(v), mask[:, None, None], sinks)
    return o.reshape(bsz, seqlen, B_Q_HEADS * B_HEAD_DIM)


def window_attention_sample(q, k_new, v_new, k_buf, v_buf, sinks):
    bsz, t = q.shape[:2]
    nbuf = k_buf.shape[1]
    kk = jnp.concatenate([k_buf.astype(k_new.dtype), k_new], axis=1)
    vv = jnp.concatenate([v_buf.astype(v_new.dtype), v_new], axis=1)
    qpos = PAST_LEN + jnp.arange(t)
    kpos = jnp.concatenate([PAST_LEN - nbuf + jnp.arange(nbuf), PAST_LEN + jnp.arange(t)])
    rel = qpos[:, None] - kpos[None, :]
    mask = (rel >= 0) & (rel < WINDOW)
    o = sink_attention(q, kk, vv, mask, sinks)
    return o.reshape(bsz, t, B_Q_HEADS * B_HEAD_DIM)


def swiglu(h, g, w_gu, w_down):
    gt, up = jnp.split(rmsnorm(h, g) @ w_gu, 2, axis=-1)
    return (jax.nn.silu(gt) * up) @ w_down


def ple_term(h, pe, g, w_proj, w_gate):
    gate = jax.nn.sigmoid(rmsnorm(h, g) @ w_gate)
    return (pe.astype(h.dtype) @ w_proj) * gate


def run_trunk(x, pe, s_init, c_init, k_buf, v_buf, win_buf, prm):
    bsz, seqlen, _ = x.shape
    h = x
    s_out, c_out = [], []
    k_sh = None
    v_sh = None
    for i in range(DEPTH):
        if i < N_A_LAYERS:
            mix, s_new, c_new = gated_delta_mixer(
                h, s_init[i], c_init[i], prm['a_norm'][i], prm['a_w_in'][i], prm['a_conv_w'][i],
                prm['a_a_log'][i], prm['a_dt_bias'][i], prm['a_out_norm'][i], prm['a_w_out'][i])
            s_out.append(s_new)
            c_out.append(c_new)
        else:
            if i == N_A_LAYERS:
                kv = (rmsnorm(h, prm['kv_norm']) @ prm['kv_w']).reshape(bsz, seqlen, 2, B_KV_HEADS, B_HEAD_DIM)
                k_sh = kv[:, :, 0]
                v_sh = kv[:, :, 1]
            j = i - N_A_LAYERS
            q = (rmsnorm(h, prm['b_norm'][j]) @ prm['b_w_q'][j]).reshape(
                bsz, seqlen, B_KV_HEADS, B_GROUP, B_HEAD_DIM) * (B_HEAD_DIM ** -0.5)
            if k_buf is None:
                o = window_attention_prompt(q, k_sh, v_sh, prm['b_sinks'][j])
            else:
                o = window_attention_sample(q, k_sh, v_sh, k_buf, v_buf, prm['b_sinks'][j])
            mix = o @ prm['b_w_o'][j]
        h = h + mix
        h = h + swiglu(h, prm['ffn_norm'][i], prm['ffn_w_gu'][i], prm['ffn_w_down'][i])
        h = h + ple_term(h, pe[i], prm['ple_norm'][i], prm['ple_w_proj'][i], prm['ple_w_gate'][i])
    y = rmsnorm(h, prm['final_norm'])
    if k_buf is None:
        k_win = k_sh[:, -win_buf:]
        v_win = v_sh[:, -win_buf:]
    else:
        k_win = jnp.concatenate([k_buf.astype(k_sh.dtype), k_sh], axis=1)[:, -win_buf:]
        v_win = jnp.concatenate([v_buf.astype(v_sh.dtype), v_sh], axis=1)[:, -win_buf:]
    return y, jnp.stack(s_out), jnp.stack(c_out), k_win, v_win


def setup_inputs(seed: int = 0) -> dict:
    key = jax.random.key(seed)
    ks = jax.random.split(key, 32)
    D = D_MODEL
    win_buf = min(WINDOW, PAST_LEN)

    def nrm(i, shape, scale):
        return jax.random.normal(ks[i], shape, jnp.float32) * scale

    def gain(i, shape):
        return 1.0 + nrm(i, shape, 0.02)

    dt = jnp.exp(jax.random.uniform(ks[22], (N_A_LAYERS, A_HEADS), jnp.float32,
                                    minval=math.log(1e-3), maxval=math.log(1e-1)))
    return {
        'x_prompt': nrm(0, (BATCH, SEQ, D), 1.0),
        'x_sample': nrm(1, (DEC_BATCH, DEC_SEQ, D), 1.0),
        'state_delta': nrm(2, (N_A_LAYERS, DEC_BATCH, A_HEADS, A_DK, A_DV), 0.1),
        'state_conv': nrm(3, (N_A_LAYERS, DEC_BATCH, CONV_W - 1, A_CONV_CH), 1.0),
        'cache_k_win': nrm(4, (DEC_BATCH, win_buf, B_KV_HEADS, B_HEAD_DIM), 1.0),
        'cache_v_win': nrm(5, (DEC_BATCH, win_buf, B_KV_HEADS, B_HEAD_DIM), 1.0),
        'p_prompt': nrm(6, (DEPTH, BATCH, SEQ, PLE_DIM), 1.0),
        'p_sample': nrm(7, (DEPTH, DEC_BATCH, DEC_SEQ, PLE_DIM), 1.0),
        'a_norm': gain(8, (N_A_LAYERS, D)),
        'a_w_in': nrm(9, (N_A_LAYERS, D, A_IN), D ** -0.5),
        'a_conv_w': nrm(10, (N_A_LAYERS, CONV_W, A_CONV_CH), CONV_W ** -0.5),
        'a_a_log': jnp.log(jax.random.uniform(ks[11], (N_A_LAYERS, A_HEADS), jnp.float32, minval=1.0, maxval=16.0)),
        'a_dt_bias': dt + jnp.log(-jnp.expm1(-dt)),
        'a_out_norm': gain(12, (N_A_LAYERS, A_DV)),
        'a_w_out': nrm(13, (N_A_LAYERS, A_VD, D), A_VD ** -0.5),
        'kv_norm': gain(14, (D,)),
        'kv_w': nrm(15, (D, 2 * B_KV_HEADS * B_HEAD_DIM), D ** -0.5),
        'b_norm': gain(16, (N_B_LAYERS, D)),
        'b_w_q': nrm(17, (N_B_LAYERS, D, B_Q_HEADS * B_HEAD_DIM), D ** -0.5),
        'b_sinks': nrm(18, (N_B_LAYERS, B_Q_HEADS), 1.0),
        'b_w_o': nrm(19, (N_B_LAYERS, B_Q_HEADS * B_HEAD_DIM, D), (B_Q_HEADS * B_HEAD_DIM) ** -0.5),
        'ffn_norm': gain(20, (DEPTH, D)),
        'ffn_w_gu': nrm(21, (DEPTH, D, 2 * FFN_HIDDEN), D ** -0.5),
        'ffn_w_down': nrm(23, (DEPTH, FFN_HIDDEN, D), FFN_HIDDEN ** -0.5),
        'ple_norm': gain(24, (DEPTH, D)),
        'ple_w_proj': nrm(25, (DEPTH, PLE_DIM, D), PLE_DIM ** -0.5),
        'ple_w_gate': nrm(26, (DEPTH, D, D), D ** -0.5),
        'final_norm': gain(27, (D,)),
    }


def reference(x_prompt, x_sample, state_delta, state_conv, cache_k_win, cache_v_win, p_prompt, p_sample,
              a_norm, a_w_in, a_conv_w, a_a_log, a_dt_bias, a_out_norm, a_w_out,
              kv_norm, kv_w, b_norm, b_w_q, b_sinks, b_w_o,
              ffn_norm, ffn_w_gu, ffn_w_down, ple_norm, ple_w_proj, ple_w_gate, final_norm):
    prm = {
        'a_norm': a_norm, 'a_w_in': a_w_in, 'a_conv_w': a_conv_w, 'a_a_log': a_a_log,
        'a_dt_bias': a_dt_bias, 'a_out_norm': a_out_norm, 'a_w_out': a_w_out,
        'kv_norm': kv_norm, 'kv_w': kv_w, 'b_norm': b_norm, 'b_w_q': b_w_q,
        'b_sinks': b_sinks, 'b_w_o': b_w_o, 'ffn_norm': ffn_norm, 'ffn_w_gu': ffn_w_gu,
        'ffn_w_down': ffn_w_down, 'ple_norm': ple_norm, 'ple_w_proj': ple_w_proj,
        'ple_w_gate': ple_w_gate, 'final_norm': final_norm,
    }
    win_buf = cache_k_win.shape[1]
    bsz = x_prompt.shape[0]
    s0 = jnp.zeros((N_A_LAYERS, bsz, A_HEADS, A_DK, A_DV), jnp.float32)
    c0 = jnp.zeros((N_A_LAYERS, bsz, CONV_W - 1, A_CONV_CH), x_prompt.dtype)
    y_prompt, sd_p, sc_p, kw_p, vw_p = run_trunk(x_prompt, p_prompt, s0, c0, None, None, win_buf, prm)
    y_sample, sd_s, sc_s, kw_s, vw_s = run_trunk(x_sample, p_sample, state_delta, state_conv,
                                                 cache_k_win, cache_v_win, win_buf, prm)
    return (y_prompt, y_sample, sd_p, sd_s, sc_p, sc_s, kw_p, kw_s, vw_p, vw_s)
```

```python
import contextlib
import numpy as np
import concourse.bass as bass
import concourse.mybir as mybir
from concourse.bass_utils import run_bass_kernel_spmd

F32 = mybir.dt.float32
BF16 = mybir.dt.bfloat16
AF = mybir.ActivationFunctionType
ALU = mybir.AluOpType
AX = mybir.AxisListType

D = 1024
SEQ = 2048
NH = 8
FF = 2816
NFC = 22
EPS = 1e-6
NEG = -1e30
SEM_LIM = 30000


class StopBuild(Exception):
    pass


class V:
    __slots__ = ("ap", "keys")

    def __init__(self, ap, keys):
        self.ap = ap
        self.keys = tuple(keys)

    def __getitem__(self, idx):
        return V(self.ap[idx], self.keys)

    def bc(self, shape):
        return V(self.ap.to_broadcast(list(shape)), self.keys)

    def un(self, axis):
        return V(self.ap.unsqueeze(axis), self.keys)

    def re(self, pat, **kw):
        return V(self.ap.rearrange(pat, **kw), self.keys)


class VC(V):
    __slots__ = ("name", "n")

    def __init__(self, ap, name, n):
        V.__init__(self, ap, tuple(f"{name}{i}" for i in range(n)))
        self.name = name
        self.n = n

    def __getitem__(self, idx):
        keys = self.keys
        if isinstance(idx, tuple) and len(idx) >= 2:
            i1 = idx[1]
            if isinstance(i1, int):
                keys = (f"{self.name}{i1}",)
            elif isinstance(i1, slice):
                keys = tuple(f"{self.name}{i}" for i in range(*i1.indices(self.n)))
        return V(self.ap[idx], keys)


class Op:
    __slots__ = ("eng", "fn", "deps", "signaled", "is_dma", "sem", "val", "idx")

    def __init__(self, eng, fn):
        self.eng = eng
        self.fn = fn
        self.deps = []
        self.signaled = False
        self.is_dma = False
        self.sem = None
        self.val = 0


class Ring:
    def __init__(self, views):
        self.views = views
        self.i = 0

    def next(self):
        v = self.views[self.i % len(self.views)]
        self.i += 1
        return v


class Prog:
    ENGS = ("pe", "act", "dve", "pool", "sp")

    def __init__(self, nc, es):
        self.nc = nc
        self.es = es
        self.ops = {e: [] for e in self.ENGS}
        self.lastw = {}
        self.readers = {}
        self.dma_sems = {}
        self.dma_cnt = {}
        self.dma_last = {}
        self.nops = 0
        self.psum_keys = set()

    def op(self, eng, fn, reads=(), writes=()):
        o = Op(eng, fn)
        deps = {}
        for k in reads:
            w = self.lastw.get(k)
            if w is not None:
                deps[id(w)] = w
            if k in self.psum_keys:
                for r in self.readers.get(k, ()):
                    if r.eng != eng:
                        deps[id(r)] = r
        for k in writes:
            w = self.lastw.get(k)
            if w is not None:
                deps[id(w)] = w
            for r in self.readers.get(k, ()):
                deps[id(r)] = r
        for d in deps.values():
            if d.eng == "pe" and eng == "pe" and not d.is_dma:
                continue
            o.deps.append(d)
            d.signaled = True
        for k in reads:
            self.readers.setdefault(k, []).append(o)
        for k in writes:
            self.lastw[k] = o
            self.readers[k] = []
        self.ops[eng].append(o)
        self.nops += 1
        return o

    def dma(self, queue, pairs, key, reads=(), writes=()):
        if key not in self.dma_sems:
            self.dma_sems[key] = self.es.enter_context(self.nc.semaphore("d_" + key))
            self.dma_cnt[key] = 0
        sem = self.dma_sems[key]
        deps = {}
        for k in reads:
            w = self.lastw.get(k)
            if w is not None:
                deps[id(w)] = w
        for k in writes:
            w = self.lastw.get(k)
            if w is not None:
                deps[id(w)] = w
            for r in self.readers.get(k, ()):
                deps[id(r)] = r
        prev = self.dma_last.get(key)
        if prev is not None:
            deps[id(prev)] = prev
        for d in deps.values():
            d.signaled = True
        self.dma_cnt[key] += 16 * len(pairs)
        val = self.dma_cnt[key]
        last = None
        for i, (o_ap, i_ap) in enumerate(pairs):
            def fn(e, o_ap=o_ap, i_ap=i_ap):
                return e.dma_start(out=o_ap, in_=i_ap)
            o = Op(queue, fn)
            o.deps = list(deps.values()) if i == 0 else []
            o.is_dma = True
            o.sem = sem
            o.val = val
            self.ops[queue].append(o)
            self.nops += 1
            last = o
        for k in reads:
            self.readers.setdefault(k, []).append(last)
        for k in writes:
            self.lastw[k] = last
            self.readers[k] = []
        self.dma_last[key] = last
        return last

    def mm(self, out, lhsT, rhs, start=True, stop=True):
        self.op("pe", lambda e: e.matmul(out.ap, lhsT=lhsT.ap, rhs=rhs.ap, start=start, stop=stop),
                reads=lhsT.keys + rhs.keys, writes=out.keys)

    def act(self, out, in_, func, bias=None, scale=None, accum=None):
        kw = {}
        reads = list(in_.keys)
        writes = list(out.keys)
        if bias is not None:
            if isinstance(bias, V):
                kw["bias"] = bias.ap
                reads += bias.keys
            else:
                kw["bias"] = float(bias)
        if scale is not None:
            if isinstance(scale, V):
                kw["scale"] = scale.ap
                reads += scale.keys
            else:
                kw["scale"] = float(scale)
        if accum is not None:
            kw["accum_out"] = accum.ap
            writes += accum.keys
        self.op("act", lambda e: e.activation(out=out.ap, in_=in_.ap, func=func, **kw), reads, writes)

    def ts(self, eng, out, in0, s1, s2, op0, op1=None):
        reads = list(in0.keys)
        a1 = s1
        a2 = s2
        if isinstance(s1, V):
            a1 = s1.ap
            reads += s1.keys
        if isinstance(s2, V):
            a2 = s2.ap
            reads += s2.keys
        kw = {}
        if op1 is not None:
            kw["op1"] = op1

        def fn(e):
            return e.tensor_scalar(out=out.ap, in0=in0.ap, scalar1=a1, scalar2=a2, op0=op0, **kw)
        self.op(eng, fn, reads, out.keys)

    def stt(self, eng, out, in0, scalar, in1, op0, op1):
        reads = list(in0.keys) + list(in1.keys)
        a = scalar
        if isinstance(scalar, V):
            a = scalar.ap
            reads += scalar.keys
        self.op(eng, lambda e: e.scalar_tensor_tensor(out=out.ap, in0=in0.ap, scalar=a, in1=in1.ap, op0=op0, op1=op1),
                reads, out.keys)

    def tt(self, eng, out, in0, in1, op):
        self.op(eng, lambda e: e.tensor_tensor(out=out.ap, in0=in0.ap, in1=in1.ap, op=op),
                in0.keys + in1.keys, out.keys)

    def red(self, eng, out, in_, op):
        self.op(eng, lambda e: e.tensor_reduce(out=out.ap, in_=in_.ap, axis=AX.X, op=op), in_.keys, out.keys)

    def copy(self, eng, out, in_, scale=None):
        if eng == "act":
            self.act(out, in_, AF.Copy, scale=scale)
        else:
            assert scale is None
            self.op(eng, lambda e: e.tensor_copy(out=out.ap, in_=in_.ap), in_.keys, out.keys)

    def recip(self, out, in_):
        self.op("dve", lambda e: e.reciprocal(out=out.ap, in_=in_.ap), in_.keys, out.keys)

    def memset(self, eng, out, val):
        self.op(eng, lambda e: e.memset(out.ap, val), (), out.keys)

    def emit(self):
        nc, es = self.nc, self.es
        sems = {}
        for eng in ("pe", "act", "dve", "pool"):
            n = 0
            for o in self.ops[eng]:
                if o.is_dma:
                    continue
                if o.signaled:
                    o.idx = n
                    n += 1
            nsem = n // SEM_LIM + 1
            sems[eng] = [es.enter_context(nc.semaphore(f"s_{eng}{i}")) for i in range(nsem)]
            for o in self.ops[eng]:
                if not o.is_dma and o.signaled:
                    o.sem = sems[eng][o.idx // SEM_LIM]
                    o.val = o.idx % SEM_LIM + 1
        block = es.enter_context(nc.Block())
        final = [(self.dma_sems[k], self.dma_cnt[k]) for k in self.dma_sems]

        def run(name, e):
            waited = {}
            for o in self.ops[name]:
                for d in o.deps:
                    sid = id(d.sem)
                    if waited.get(sid, 0) < d.val:
                        e.wait_ge(d.sem, d.val)
                        waited[sid] = d.val
                ins = o.fn(e)
                if o.is_dma:
                    ins.then_inc(o.sem, 16)
                elif o.signaled:
                    ins.then_inc(o.sem, 1)
            if name == "sp":
                for (s, v) in final:
                    if waited.get(id(s), 0) < v:
                        e.wait_ge(s, v)

        @block.tensor
        def _(e):
            run("pe", e)

        @block.scalar
        def _(e):
            run("act", e)

        @block.vector
        def _(e):
            run("dve", e)

        @block.gpsimd
        def _(e):
            run("pool", e)

        @block.sync
        def _(e):
            run("sp", e)


def make_consts():
    i = np.arange(128)
    I, J = np.meshgrid(i, i, indexing="ij")
    c = {}
    c["ident"] = (I == J).astype(np.float32)
    c["tri_p"] = (I <= J).astype(np.float32)
    c["seg_p"] = np.ones((128, 128), np.float32)

    def lev(a, b):
        l = np.full(a.shape, 3)
        l[(a // 64) == (b // 64)] = 2
        l[(a // 32) == (b // 32)] = 1
        l[(a // 16) == (b // 16)] = 0
        return l
    L = lev(I, J)
    lower = I > J
    upper = I < J
    c["mL_p"] = np.stack([(lower & (L == k)).astype(np.float32) for k in range(4)], 1)
    c["mU_p"] = np.stack([(upper & (L == k)).astype(np.float32) for k in range(4)], 1)
    c["incU_p"] = (I <= J).astype(np.float32)
    same = (I // 8) == (J // 8)
    c["tri_s"] = ((I <= J) & same).astype(np.float32)
    c["seg_s"] = same.astype(np.float32)
    z = np.zeros((128, 128), np.float32)
    c["mL_s"] = (lower & same).astype(np.float32)
    c["mU_s"] = (upper & same).astype(np.float32)
    c["incU_s"] = ((I <= J) & same).astype(np.float32)
    sg = np.zeros((128, 16), np.float32)
    sg[i, i // 8] = 1.0
    c["segmask"] = sg
    q = np.arange(128)[:, None]
    k = np.arange(256)[None, :]
    rel = q + 128 - k
    ok = (rel >= 0) & (rel < 128)
    c["amask"] = np.where(ok, 0.0, NEG).astype(np.float32)
    ok0 = ok & (k >= 128)
    c["amask0"] = np.where(ok0, 0.0, NEG).astype(np.float32)
    s_q = (np.arange(128) // 8)[:, None]
    t_q = (np.arange(128) % 8)[:, None]
    s_k = (np.arange(2048) // 128)[None, :]
    n_k = (np.arange(2048) % 128)[None, :]
    okc = (s_k == s_q) & (n_k > t_q)
    s_n = (np.arange(128) // 8)[None, :]
    t_n = (np.arange(128) % 8)[None, :]
    okn = (s_n == s_q) & (t_n <= t_q)
    c["smask"] = np.where(np.concatenate([okc, okn], 1), 0.0, NEG).astype(np.float32)
    return c


CONST_ORDER = ["ident", "tri_p", "seg_p", "mL_p", "mU_p", "incU_p", "tri_s", "seg_s", "mL_s", "mU_s", "incU_s",
               "segmask", "amask", "amask0"]


def pack_consts():
    c = make_consts()
    offs = {}
    cols = []
    o = 0
    for k in CONST_ORDER:
        a = c[k].reshape(128, -1)
        offs[k] = (o, a.shape[1])
        cols.append(a)
        o += a.shape[1]
    return np.ascontiguousarray(np.concatenate(cols, 1)), offs, np.ascontiguousarray(c["smask"])


def build(cfg):
    NSEQ = cfg["nseq"]
    TPS = cfg["tps"]
    SAMPLE = cfg["sample"]
    NTOK = NSEQ * TPS * 128
    nc = bass.Bass("TRN2", target_bir_lowering=False)
    cpack, coffs, smask_np = pack_consts()
    NCC = cpack.shape[1]

    def din(name, shape):
        return nc.dram_tensor(name, list(shape), F32, kind="ExternalInput").ap()

    def dout(name, shape):
        return nc.dram_tensor(name, list(shape), F32, kind="ExternalOutput").ap()

    x_p = din("x_p", [max(NTOK, 1), D])
    pe_p = din("pe_p", [2, max(NTOK, 1), 256])
    x_s = din("x_s", [128, D])
    pe_s = din("pe_s", [2, 128, 256])
    sd_in = din("sd_in", [16, NH, 128, 128])
    sc_in = din("sc_in", [48, 3072])
    ck_in = din("ck_in", [16, 128, 256])
    cv_in = din("cv_in", [16, 128, 256])
    w_in = din("w_in", [D, 4112])
    w_out = din("w_out", [D, D])
    w_gu = din("w_gu", [2, D, 2 * FF])
    w_down = din("w_down", [2, FF, D])
    w_pp = din("w_pp", [2, 256, D])
    w_pg = din("w_pg", [2, D, D])
    w_kv = din("w_kv", [D, 512])
    w_q = din("w_q", [D, D])
    w_o = din("w_o", [D, D])
    packA = din("packA", [65, 128])
    packB = din("packB", [96, 128])
    vec8 = din("vec8", [1, 32])
    cst = din("cst", [128, NCC])
    smask_d = din("smask", [128, 2176])

    y_p = dout("y_p", [max(NTOK, 1), D])
    y_s = dout("y_s", [128, D])
    sd_p = dout("sd_p", [max(NSEQ, 1), NH, 128, 128])
    sd_s = dout("sd_s", [16, NH, 128, 128])
    sc_p = dout("sc_p", [max(NSEQ, 1) * 3, 3072])
    sc_s = dout("sc_s", [48, 3072])
    kw_p = dout("kw_p", [max(NSEQ, 1), 128, 256])
    kw_s = dout("kw_s", [16, 128, 256])
    vw_p = dout("vw_p", [max(NSEQ, 1), 128, 256])
    vw_s = dout("vw_s", [16, 128, 256])

    es = contextlib.ExitStack()
    with es:
        P = Prog(nc, es)

        def sb(name, shape, dt=F32, keys=None):
            t = es.enter_context(nc.sbuf_tensor(name, list(shape), dt))
            return V(t[:], keys if keys is not None else (name,))

        def ring(name, n, shape, dt=F32):
            return Ring([sb(f"{name}{i}", shape, dt) for i in range(n)])

        def psring(name, n, width=512):
            views = []
            for i in range(n):
                t = es.enter_context(nc.psum_tensor(f"{name}{i}", [128, 512], F32))
                views.append(V(t[:, 0:width], (f"{name}{i}",)))
                P.psum_keys.add(f"{name}{i}")
            return Ring(views)

        C = sb("C", [128, NCC])

        def cv(name):
            o, n = coffs[name]
            return C[:, o:o + n]
        identF = cv("ident")
        identB = sb("identB", [128, 128], BF16)
        ones1024 = sb("ones1024", [128, 128], BF16)
        ones128 = sb("ones128", [128, 128], BF16)
        ones1 = sb("ones1", [128, 128], BF16)
        kc = sb("kc", [128, 8])
        gains = sb("gains", [128, 65])
        convw = sb("convw", [128, 96])
        v8 = sb("v8", [128, 32])
        negA = sb("negA", [128, 8])
        h = VC(sb("h", [128, 8, 512]).ap, "h", 8)
        xn = VC(sb("xn", [128, 8, 512], BF16).ap, "xn", 8)
        sqr = ring("sqc", 2, [128, 512], BF16)
        rs = sb("rs", [128, 512])
        peT = [sb(f"peT{l}", [128, 2, 512], BF16) for l in range(2)]
        og = sb("og", [128, 8, 512], BF16)
        bigt = es.enter_context(nc.sbuf_tensor("big", [128, 6144], F32))
        big32 = V(bigt[:], ("big",))
        big16 = V(bigt[:].bitcast(BF16), ("big",))
        actb = big16[:, 0:NFC * 512].re("p (c n) -> p c n", c=NFC)
        S32 = sb("S32", [128, 8, 128])
        S16 = sb("S16", [128, 8, 128], BF16)
        cc = sb("cc", [128, 24, 3])
        wab = sb("wab", [128, 8, 16], BF16)
        wslots = ring("ws", 3, [128, 4096], BF16)
        xst = ring("xst", 2, [128, 1024])
        pst = ring("pst", 2, [128, 256])
        pb = psring("pb", 8)
        mmr = pb

        class _Sm:
            def next(self):
                return pb.next()[:, 0:128]
        smr = _Sm()
        msr = smr

        GI = {"a_norm": 0, "ffn0": 1, "ffn1": 2, "ple0": 3, "ple1": 4, "kv": 5, "b": 6, "fin": 7}

        P.dma("sp", [(C.ap, cst)], "C", writes=C.keys)
        P.memset("dve", kc[:, 0:1], 0.0)
        P.memset("dve", kc[:, 1:2], -1.0)
        P.memset("dve", kc[:, 2:3], 1.0)
        P.memset("dve", ones1024, 1.0 / 1024.0)
        P.memset("dve", ones128, 1.0 / 128.0)
        P.memset("dve", ones1, 1.0)
        P.copy("dve", identB, identF)
        xs0 = xst.next()
        P.dma("sp", [(xs0.ap[0:65, 0:128], packA)], "xst0", writes=xs0.keys)
        ps = smr.next()
        P.mm(ps[:, 0:65], lhsT=xs0[0:65, 0:128], rhs=identF[0:65, 0:65])
        P.copy("dve", gains, ps[:, 0:65])
        xs1 = xst.next()
        P.dma("sp", [(xs1.ap[0:96, 0:128], packB)], "xst1", writes=xs1.keys)
        ps = smr.next()
        P.mm(ps[:, 0:96], lhsT=xs1[0:96, 0:128], rhs=identF[0:96, 0:96])
        P.copy("dve", convw, ps[:, 0:96])
        P.dma("sp", [(v8.ap, vec8.partition_broadcast(128))], "v8", writes=v8.keys)
        sinkp = sb("sinkp", [128, 16])
        negsink = sb("negsink", [128, 16])
        for k_ in range(4):
            for b_ in range(2):
                P.copy("dve", sinkp[:, 4 * k_ + 2 * b_:4 * k_ + 2 * b_ + 2],
                       v8[:, 16 + 4 * k_:20 + 4 * k_].re("p (a b) -> p a b", b=2)[:, :, b_])
        P.ts("dve", negsink, sinkp, -1.0, None, ALU.mult)
        P.act(negA, v8[:, 0:8], AF.Exp)
        P.ts("dve", negA, negA, -1.0, None, ALU.mult)
        P.dma("pool", [(wab.ap, w_in[:, 4096:4112].rearrange("(c p) n -> p c n", p=128))], "wab", writes=wab.keys)

        WT = {}

        def wdef(name, parts):
            WT[name] = (len(WT), parts, max(p[2] for p in parts))

        def part_dst(ap, lo, hi, kcn):
            return ap[:, lo:hi].rearrange("p (c n) -> p c n", c=kcn)

        def wload(name):
            ti, _, nu = WT[name]
            slot = wslots.next()
            P.dma(cfg.get("wq", "sp"), [(slot.ap[:, 0:nu], wsc[ti][:, 0:nu])], slot.keys[0], reads=(f"wsc{ti}",), writes=slot.keys)
            return slot

        def wsrc(w2d, c0, ncols):
            return w2d[:, c0:c0 + ncols].rearrange("(c p) n -> p c n", p=128)

        def rms(N):
            ps = mmr.next()
            for c in range(8):
                sqc = sqr.next()
                P.act(sqc[:, :N], h[:, c, :N], AF.Square)
                P.mm(ps[:, :N], lhsT=ones1024, rhs=sqc[:, :N], start=(c == 0), stop=(c == 7))
            P.act(rs[:, :N], ps[:, :N], AF.Ln, bias=EPS)
            P.act(rs[:, :N], rs[:, :N], AF.Exp, scale=-0.5)

        def normed(dst, gi, N):
            for c in range(8):
                P.stt("dve", dst[:, c, :N], h[:, c, :N], gains[:, gi * 8 + c:gi * 8 + c + 1], rs[:, :N],
                      ALU.mult, ALU.mult)

        R = {}
        t512 = ring("t512", 3, [128, 512])

        def wt(role, n=2, shape=(128, 128), dt=F32):
            if role not in R:
                R[role] = ring("r_" + role, n, list(shape), dt)
            return R[role].next()

        def ffn(l, N):
            rms(N)
            normed(xn, GI[f"ffn{l}"], N)
            for jj in range(NFC // 2):
                slot = wload(f"gu{l}_{jj}")
                for jo in range(2):
                    j = jj * 2 + jo
                    psg = mmr.next()
                    for c in range(8):
                        P.mm(psg[:, :N], lhsT=slot[:, c * 256 + jo * 128:c * 256 + jo * 128 + 128], rhs=xn[:, c, :N],
                             start=(c == 0), stop=(c == 7))
                    psu = mmr.next()
                    for c in range(8):
                        P.mm(psu[:, :N], lhsT=slot[:, 2048 + c * 256 + jo * 128:2048 + c * 256 + jo * 128 + 128],
                             rhs=xn[:, c, :N], start=(c == 0), stop=(c == 7))
                    sg = t512.next()
                    P.act(sg[:, :N], psg[:, :N], AF.Silu)
                    P.tt("dve", actb[:, j, :N], sg[:, :N], psu[:, :N], ALU.mult)
            for m in range(8):
                slot = wload(f"dn{l}_{m}")
                ps = mmr.next()
                for c in range(NFC):
                    P.mm(ps[:, :N], lhsT=slot[:, c * 128:(c + 1) * 128], rhs=actb[:, c, :N],
                         start=(c == 0), stop=(c == NFC - 1))
                P.tt("dve", h[:, m, :N], h[:, m, :N], ps[:, :N], ALU.add)

        def ple(l, N):
            rms(N)
            normed(xn, GI[f"ple{l}"], N)
            pslot = wload(f"pp{l}")
            for half in range(2):
                slot = wload(f"pg{l}_{half}")
                for mi in range(4):
                    m = half * 4 + mi
                    psg = mmr.next()
                    for c in range(8):
                        P.mm(psg[:, :N], lhsT=slot[:, c * 512 + mi * 128:c * 512 + mi * 128 + 128], rhs=xn[:, c, :N],
                             start=(c == 0), stop=(c == 7))
                    psp = mmr.next()
                    for c in range(2):
                        P.mm(psp[:, :N], lhsT=pslot[:, c * 1024 + m * 128:c * 1024 + m * 128 + 128],
                             rhs=peT[l][:, c, :N], start=(c == 0), stop=(c == 1))
                    sg = t512.next()
                    P.act(sg[:, :N], psg[:, :N], AF.Sigmoid)
                    P.tt("dve", sg[:, :N], sg[:, :N], psp[:, :N], ALU.mult)
                    P.tt("dve", h[:, m, :N], h[:, m, :N], sg[:, :N], ALU.add)

        def bank(w=512):
            return pb.next()[:, 0:w]

        def bmm(ps, lhs, rhs, T):
            for t in range(T):
                P.mm(ps[:, t * 128:(t + 1) * 128], lhsT=lhs(t), rhs=rhs(t))

        def inverse_upper(W, T, sample):
            nsq = 2 if sample else 3
            Lm, Um = W["Lm"], W["Um"]
            TW = T * 128

            def m3(v):
                return v[:, 0:TW].re("p (t n) -> p t n", t=T)

            def lv(M, lev):
                return lambda t: M[:, t, lev, :]

            def tl(v):
                return lambda t: v[:, t * 128:(t + 1) * 128]
            identT = identF.un(1).bc([128, T, 128])
            Ls = [lv(Lm, 0)]
            Us = [lv(Um, 0)]
            for k in range(nsq):
                needU = (k < nsq - 1) or not sample
                psL = bank(TW)
                bmm(psL, Us[k], Ls[k], T)
                if needU:
                    psU = bank(TW)
                    bmm(psU, Ls[k], Us[k], T)
                Ln = W[f"L{k + 1}"]
                P.copy("act", Ln[:, 0:TW], psL)
                Ls.append(tl(Ln))
                if needU:
                    Un = W[f"U{k + 1}"]
                    P.copy("dve", Un[:, 0:TW], psU)
                    Us.append(tl(Un))
                yield
            pu = [W["PUa"], W["PUb"]]
            pl = [W["PLa"], W["PLb"]] if not sample else None
            pi = 0
            PU = pu[0]
            P.tt(cfg.get("ce", "pool"), m3(PU), identT, Um[:, :, 0, :], ALU.subtract)
            if not sample:
                PL = pl[0]
                P.tt(cfg.get("ce", "pool"), m3(PL), identT, Lm[:, :, 0, :], ALU.subtract)
            for k in range(1, nsq + 1):
                psU = bank(TW)
                bmm(psU, Ls[k], tl(PU), T)
                if not sample:
                    psL = bank(TW)
                    bmm(psL, Us[k], tl(PL), T)
                PUn = W["TU"] if (sample and k == nsq) else pu[1 - pi]
                P.tt("dve", PUn[:, 0:TW], PU[:, 0:TW], psU, ALU.add)
                if not sample:
                    PLn = pl[1 - pi]
                    P.tt("dve", PLn[:, 0:TW], PL[:, 0:TW], psL, ALU.add)
                    PL = PLn
                pi = 1 - pi
                PU = PUn
                yield
            if sample:
                return
            Z, Z2 = W["Z"], W["Z2"]
            for lev in (1, 2, 3):
                last = lev == 3
                psa = bank(TW)
                bmm(psa, lv(Lm, lev), tl(PU), T)
                if not last:
                    psb = bank(TW)
                    bmm(psb, lv(Um, lev), tl(PL), T)
                P.copy("act", Z[:, 0:TW], psa)
                if not last:
                    P.copy("dve", Z2[:, 0:TW], psb)
                yield
                psa = bank(TW)
                bmm(psa, tl(PL), tl(Z), T)
                if not last:
                    psb = bank(TW)
                    bmm(psb, tl(PU), tl(Z2), T)
                PUn = W["TU"] if last else pu[1 - pi]
                P.tt("dve", PUn[:, 0:TW], PU[:, 0:TW], psa, ALU.subtract)
                if not last:
                    PLn = pl[1 - pi]
                    P.tt("dve", PLn[:, 0:TW], PL[:, 0:TW], psb, ALU.subtract)
                    PL = PLn
                pi = 1 - pi
                PU = PUn
                yield

        def layer_a(kind, seq, N, nt, first, last_g):
            sample = kind == "s"
            sfx = "_s" if sample else "_p"
            tri, seg = cv("tri" + sfx), cv("seg" + sfx)
            NL = 1 if sample else 4
            mLc = cv("mL" + sfx).re("p (l n) -> p l n", l=NL)
            mUc = cv("mU" + sfx).re("p (l n) -> p l n", l=NL)
            incU = cv("incU" + sfx)
            rms(N)
            normed(xn, GI["a_norm"], N)
            gtok = wt("gtok", 1, (128, 4, 8))
            btok = wt("btok", 1, (128, 4, 8))
            gct = wt("gct", 1, (128, 4, 8))
            dlt = wt("dlt", 1, (128, 4, 8))
            glt = wt("glt", 1, (128, 4, 8))
            ps = bank()
            for t in range(nt):
                for c in range(8):
                    P.mm(ps[:, t * 16:(t + 1) * 16], lhsT=xn[:, c, t * 128:(t + 1) * 128], rhs=wab[:, c, :],
                         start=(c == 0), stop=(c == 7))
            ps3 = ps[:, 0:nt * 16].re("p (t n) -> p t n", t=nt)
            tb = wt("tb4", 1, (128, 4, 8))[:, 0:nt, :]
            P.act(tb, ps3[:, :, 8:16], AF.Exp, scale=-1.0)
            P.ts("dve", tb, tb, 1.0, None, ALU.add)
            P.recip(btok[:, 0:nt, :], tb)
            xa = wt("xa4", 1, (128, 4, 8))[:, 0:nt, :]
            P.tt("dve", xa, ps3[:, :, 0:8], v8[:, 8:16].un(1).bc([128, nt, 8]), ALU.add)
            mxa = wt("mxa4", 1, (128, 4, 8))[:, 0:nt, :]
            P.ts("dve", mxa, xa, 0.0, None, ALU.max)
            ax = wt("ax4", 1, (128, 4, 8))[:, 0:nt, :]
            P.stt("dve", ax, mxa, -2.0, xa, ALU.mult, ALU.add)
            P.act(ax, ax, AF.Exp)
            P.act(ax, ax, AF.Ln, bias=1.0)
            P.tt("dve", xa, mxa, ax, ALU.add)
            P.tt("dve", gtok[:, 0:nt, :], xa, negA.un(1).bc([128, nt, 8]), ALU.mult)
            ps2 = bank()
            for t in range(nt):
                P.mm(ps2[:, t * 16:t * 16 + 8], lhsT=tri, rhs=gtok[:, t, :])
                P.mm(ps2[:, t * 16 + 8:t * 16 + 16], lhsT=seg, rhs=gtok[:, t, :])
            p23 = ps2[:, 0:nt * 16].re("p (t n) -> p t n", t=nt)
            P.copy("dve", gct[:, 0:nt, :], p23[:, :, 0:8])
            P.tt("dve", dlt[:, 0:nt, :], p23[:, :, 8:16], gct[:, 0:nt, :], ALU.subtract)
            P.act(dlt[:, 0:nt, :], dlt[:, 0:nt, :], AF.Exp)
            P.act(glt[:, 0:nt, :], p23[:, :, 8:16], AF.Exp)
            if sample:
                hist = smb[:, 0:1152].re("p (c q w) -> p c q w", c=24, q=16)
                ccs = Pm32[:, 0:1152].re("p (c q w) -> p c q w", c=24, q=16)
                for piece in range(3):
                    xs = xst.next()
                    P.dma("sp", [(xs.ap[0:48, :], sc_in[:, piece * 1024:(piece + 1) * 1024])], xs.keys[0],
                          writes=xs.keys)
                    for c8 in range(8):
                        ch = piece * 8 + c8
                        ps = smr.next()
                        P.mm(ps[:, 0:48], lhsT=xs[0:48, c8 * 128:(c8 + 1) * 128], rhs=identF[0:48, 0:48])
                        P.copy("act", hist[:, ch, :, :], ps[:, 0:48].re("p (q w) -> p q w", w=3))
            T = nt
            TW = T * 128
            if "gdnW" not in R:
                Wt = {}
                IDT = BF16 if cfg.get("inv16", True) else F32

                def cast(ap):
                    if IDT == F32:
                        return ap
                    n = ap.shape[-1]
                    return ap.bitcast(BF16)[:, 0:n]
                qa32 = qall.ap.rearrange("p c n -> p (c n)").bitcast(F32)
                for i, nm in enumerate(["D", "dL", "E", "Z"]):
                    a_ = qa32[:, i * 512:(i + 1) * 512]
                    Wt[nm] = V(cast(a_) if nm == "Z" else a_, ("g_" + nm,))
                assert IDT == BF16
                Wt["Z2"] = V(qa32[:, 3 * 512:4 * 512].bitcast(BF16)[:, 512:1024], ("g_Z2",))
                pt32 = PT.ap.rearrange("p b n -> p (b n)").bitcast(F32)
                for i, nm in enumerate(["L1", "L2"]):
                    Wt[nm] = V(cast(pt32[:, i * 512:(i + 1) * 512]), ("g_" + nm,))
                sU = sb("g_sU", [128, 4, 128])
                for i, nm in enumerate(["U1", "U2", "PUa", "PUb"]):
                    Wt[nm + "_p"] = V(cast(smb.ap[:, i * 512:(i + 1) * 512]), ("g_" + nm,))
                    Wt[nm + "_s"] = V(cast(sU.ap[:, i, :]), ("g_" + nm + "s",))
                Wt["osb"] = [sb(f"g_osb{i}", [128, 512], BF16) for i in range(3)]
                Wt["TU"] = [sb(f"g_TU{i}", [128, 512], IDT) for i in range(3)]
                Wt["vb"] = [sb(f"g_vb{i}", [128, 512], IDT) for i in range(3)]
                for nm in ["qkT", "qdec", "kbd", "kdec"]:
                    Wt[nm] = [sb(f"g_{nm}{i}", [128, 512], BF16) for i in range(3)]
                Wt["LmS"] = sb("g_LmS", [128, 1, 1, 128], IDT)
                Wt["UmS"] = sb("g_UmS", [128, 1, 1, 128], IDT)
                Wt["LmP"] = V(cast(big32.ap[:, 0:2048]).rearrange("p (t l n) -> p t l n", t=4, l=4), ("g_LmP",))
                Wt["UmP"] = V(cast(big32.ap[:, 2048:4096]).rearrange("p (t l n) -> p t l n", t=4, l=4), ("g_UmP",))
                for i, nm in enumerate(["L3", "U3", "PLa", "PLb"]):
                    Wt[nm] = V(cast(big32.ap[:, 4096 + i * 512:4096 + (i + 1) * 512]), ("g_" + nm,))
                R["gdnW"] = Wt
            Wt = R["gdnW"]

            def t3(v):
                return v[:, 0:TW].re("p (t n) -> p t n", t=T)

            def setup(hd):
                par = hd % 3
                W = dict(Wt)
                for nm in ["TU", "vb", "qkT", "qdec", "kbd", "kdec", "osb"]:
                    W[nm] = Wt[nm][par]
                W["Lm"] = Wt["LmS"] if sample else Wt["LmP"]
                W["Um"] = Wt["UmS"] if sample else Wt["UmP"]
                for nm in ["U1", "U2", "PUa", "PUb"]:
                    W[nm] = Wt[nm + ("_s" if sample else "_p")]
                slot = wload(f"in{hd}")
                pre1 = wt("pre", 1, (128, 515))
                qkv = []
                for s in range(3):
                    ch = s * 8 + hd
                    if sample:
                        pre = {s: pre1[:, 0:176].re("p (q t) -> p q t", q=16)}
                    else:
                        pre = {s: pre1}
                    ps = bank()
                    for c in range(8):
                        P.mm(ps[:, :N], lhsT=slot[:, s * 1024 + c * 128:s * 1024 + (c + 1) * 128], rhs=xn[:, c, :N],
                             start=(c == 0), stop=(c == 7))
                    cvo = t512.next()
                    if sample:
                        P.copy("act", pre[s][:, :, 0:3], hist[:, ch, :, :])
                        P.copy("act", pre[s][:, :, 3:11], ps[:, :128].re("p (q t) -> p q t", t=8))
                        cvv = cvo[:, :128].re("p (q t) -> p q t", t=8)
                        P.ts("dve", cvv, pre[s][:, :, 0:8], convw[:, ch:ch + 1], None, ALU.mult)
                        for w in range(1, 4):
                            P.stt("dve", cvv, pre[s][:, :, w:w + 8], convw[:, w * 24 + ch:w * 24 + ch + 1], cvv,
                                  ALU.mult, ALU.add)
                        P.copy("act", ccs[:, ch, :, :], pre[s][:, :, 8:11])
                    else:
                        if first:
                            P.memset("dve", pre[s][:, 0:3], 0.0)
                        else:
                            P.copy("act", pre[s][:, 0:3], cc[:, ch, :])
                        P.copy("act", pre[s][:, 3:3 + N], ps[:, :N])
                        P.ts("dve", cvo[:, :N], pre[s][:, 0:N], convw[:, ch:ch + 1], None, ALU.mult)
                        for w in range(1, 4):
                            P.stt("dve", cvo[:, :N], pre[s][:, w:w + N], convw[:, w * 24 + ch:w * 24 + ch + 1],
                                  cvo[:, :N], ALU.mult, ALU.add)
                        P.copy("act", cc[:, ch, :], pre[s][:, N:N + 3])
                    P.act(cvo[:, :N], cvo[:, :N], AF.Silu)
                    if s < 2:
                        sqq = wt("sqq", 1, (128, 512), BF16)
                        P.act(sqq[:, :N], cvo[:, :N], AF.Square)
                        pss = bank()
                        P.mm(pss[:, :N], lhsT=ones1, rhs=sqq[:, :N])
                        rn = t512.next()
                        P.act(rn[:, :N], pss[:, :N], AF.Ln, bias=1e-6)
                        P.act(rn[:, :N], rn[:, :N], AF.Exp, scale=-0.5)
                        o16 = wt("qn" if s == 0 else "kn", 2, (128, 512), BF16)
                        P.stt("dve", o16[:, :N], cvo[:, :N], (128.0 ** -0.5) if s == 0 else 1.0, rn[:, :N],
                              ALU.mult, ALU.mult)
                    else:
                        o16 = wt("vn", 2, (128, 512), BF16)
                        P.copy("dve", o16[:, :N], cvo[:, :N])
                    qkv.append(o16)
                    yield
                qn, kn, vn = qkv
                ps = bank()
                for c in range(8):
                    P.mm(ps[:, :N], lhsT=slot[:, 3072 + c * 128:3072 + (c + 1) * 128], rhs=xn[:, c, :N],
                         start=(c == 0), stop=(c == 7))
                zs = wt("zs", 3, (128, 512), BF16)
                P.act(zs[:, :N], ps[:, :N], AF.Silu)
                W["zs"] = zs
                yield
                return (hd, W, qn, kn, vn)

            def halves(hd, W, qn, kn, vn):
                NHF = 2 if (T == 4 and not cfg.get("nhf1")) else 1
                Th = T // NHF
                gens = []
                Wfull = dict(W)
                for hf in range(NHF):
                    t0 = hf * Th
                    Wh = {}
                    USED = ["D", "dL", "E", "Z", "Z2", "L1", "L2", "L3", "U1", "U2", "U3", "PUa", "PUb", "PLa", "PLb",
                            "TU", "vb", "qkT", "qdec", "kbd", "kdec", "Lm", "Um"]
                    for nm in USED:
                        v = W[nm]
                        ks = tuple(k + f"_{hf}" for k in v.keys)
                        if nm in ("Lm", "Um"):
                            Wh[nm] = V(v.ap[:, t0:t0 + Th], ks)
                        elif nm in ("zs", "gls"):
                            Wh[nm] = v
                        else:
                            Wh[nm] = V(v.ap[:, t0 * 128:(t0 + Th) * 128], ks)
                        if nm not in ("zs", "gls"):
                            prevk = Wfull[nm].keys if hf > 0 else ()
                            Wfull[nm] = V(v.ap, tuple(prevk) + ks)
                    gens.append(half(hd, Wh, t0, Th, qn, kn, vn, Wfull))
                active = list(gens)
                while active:
                    for g_ in list(active):
                        try:
                            next(g_)
                        except StopIteration:
                            active.remove(g_)
                    yield
                return Wfull

            def half(hd, W, t0, T, qn, kn, vn, Wfull):
                TW = T * 128
                cs = slice(t0 * 128, t0 * 128 + TW)

                def t3(v):
                    return v[:, 0:TW].re("p (t n) -> p t n", t=T)

                def kt(t):
                    return kn[:, (t0 + t) * 128:(t0 + t + 1) * 128]

                def qt(t):
                    return qn[:, (t0 + t) * 128:(t0 + t + 1) * 128]
                ps_kk = bank(TW)
                bmm(ps_kk, kt, kt, T)
                ps_qk = bank(TW)
                bmm(ps_qk, kt, qt, T)
                ps_G = bank(TW)
                bmm(ps_G, lambda t: gtok[:, t0 + t, hd:hd + 1].bc([128, 128]), lambda t: tri, T)
                ps_B = bank(TW)
                bmm(ps_B, lambda t: btok[:, t0 + t, hd:hd + 1].bc([128, 128]), lambda t: identF, T)
                gcb = gct[:, t0:t0 + T, hd:hd + 1].bc([128, T, 128])
                bcb = btok[:, t0:t0 + T, hd:hd + 1].bc([128, T, 128])
                Dm, dL, E, Z = W["D"], W["dL"], W["E"], W["Z"]
                P.tt("dve", t3(Dm), t3(ps_G), gcb, ALU.subtract)
                P.ts("dve", dL[:, 0:TW], Dm[:, 0:TW], 0.0, None, ALU.max)
                P.act(dL[:, 0:TW], dL[:, 0:TW], AF.Exp, scale=-1.0)
                P.ts("dve", Dm[:, 0:TW], Dm[:, 0:TW], 0.0, None, ALU.min)
                P.act(Dm[:, 0:TW], Dm[:, 0:TW], AF.Exp)
                P.act(E[:, 0:TW], ps_G, AF.Exp)
                if sample:
                    gls = wt("gls", 3, (128, 16))
                    P.act(gls, ps_G[:, 0:128].re("p (s t) -> p s t", t=8)[:, :, 7], AF.Exp)
                    Wfull["gls"] = gls
                P.tt("dve", dL[:, 0:TW], dL[:, 0:TW], ps_kk, ALU.mult)
                P.tt("dve", t3(dL), t3(dL), bcb, ALU.mult)
                P.tt(cfg.get("ce", "pool"), t3(Z), t3(Dm), incU.un(1).bc([128, T, 128]), ALU.mult)
                P.tt("dve", W["qkT"][:, 0:TW], Z[:, 0:TW], ps_qk, ALU.mult)
                P.tt("dve", Dm[:, 0:TW], Dm[:, 0:TW], ps_kk, ALU.mult)
                P.tt("dve", Dm[:, 0:TW], Dm[:, 0:TW], ps_B, ALU.mult)
                P.tt(cfg.get("ce", "pool"), W["qdec"][:, 0:TW], qn[:, cs], E[:, 0:TW], ALU.mult)
                P.tt("dve", E[:, 0:TW], E[:, 0:TW], ps_B, ALU.mult)
                P.tt(cfg.get("ce", "pool"), W["kbd"][:, 0:TW], kn[:, cs], E[:, 0:TW], ALU.mult)
                yield
                Lm, Um = W["Lm"], W["Um"]
                P.tt(cfg.get("ce", "pool"), Lm, t3(dL).un(2).bc([128, T, NL, 128]), mLc.un(1).bc([128, T, NL, 128]), ALU.mult)
                P.tt(cfg.get("ce", "pool"), Um, t3(Dm).un(2).bc([128, T, NL, 128]), mUc.un(1).bc([128, T, NL, 128]), ALU.mult)
                ps = bank(TW)
                bmm(ps, kt, lambda t: identB, T)
                P.tt("dve", t3(W["kdec"]), t3(ps), dlt[:, t0:t0 + T, hd:hd + 1].bc([128, T, 128]), ALU.mult)
                ps = bank(TW)
                bmm(ps, lambda t: vn[:, (t0 + t) * 128:(t0 + t + 1) * 128], lambda t: identB, T)
                P.tt("dve", t3(W["vb"]), t3(ps), bcb, ALU.mult)
                yield
                yield from inverse_upper(W, T, sample)

            def chain(hd, W):
                tl = lambda v, t: v[:, t * 128:(t + 1) * 128]
                osb = W["osb"]
                if sample:
                    s0f = V(big32.ap[:, 0:2048].rearrange("p (s v) -> p s v", s=16), ("bg_s0f",))
                    s0b = V(big16.ap[:, 8192:10240].rearrange("p (s v) -> p s v", s=16), ("bg_s0b",))
                    src = sd_in[:, hd].rearrange("s d v -> d s v")
                    P.dma("sp", [(s0f.ap, src)], "s0f", writes=s0f.keys)
                    P.copy("act", s0b, s0f)
                    yield
                for t in range(T):
                    kbd, qdec, qkT, kdec = tl(W["kbd"], t), tl(W["qdec"], t), tl(W["qkT"], t), tl(W["kdec"], t)
                    vb, TU = tl(W["vb"], t), tl(W["TU"], t)
                    r = wt("r16", 2, (128, 128), BF16) if cfg.get("inv16", True) else wt("r", 2)
                    if not sample:
                        ps = bank(128)
                        P.mm(ps, lhsT=kbd, rhs=S16[:, hd, :])
                        P.tt("dve", r, vb, ps, ALU.subtract)
                    else:
                        ps = bank(128)
                        for s_ in range(16):
                            P.mm(ps[:, 8 * s_:8 * s_ + 8], lhsT=s0b[:, s_, :], rhs=kbd[:, 8 * s_:8 * s_ + 8])
                        rT = wt("rT", 1)
                        P.copy("act", rT, ps)
                        ps = bank(128)
                        P.mm(ps, lhsT=rT, rhs=identF)
                        P.tt("dve", r, vb, ps, ALU.subtract)
                    yield
                    ps = bank(128)
                    P.mm(ps, lhsT=TU, rhs=r)
                    u16 = wt("u16", 2, (128, 128), BF16)
                    P.copy("act", u16, ps)
                    yield
                    if not sample:
                        ps_o = bank(128)
                        P.mm(ps_o, lhsT=S16[:, hd, :], rhs=qdec, start=True, stop=False)
                        P.mm(ps_o, lhsT=u16, rhs=qkT, start=False, stop=True)
                        P.copy("act", tl(osb, t), ps_o)
                        ps = bank(128)
                        P.mm(ps, lhsT=kdec, rhs=u16)
                        P.stt("dve", S32[:, hd, :], S32[:, hd, :], glt[:, t, hd:hd + 1], ps, ALU.mult, ALU.add)
                        P.copy("dve", S16[:, hd, :], S32[:, hd, :])
                    else:
                        gls = W["gls"]
                        ps_o = bank(128)
                        for s_ in range(16):
                            P.mm(ps_o[:, 8 * s_:8 * s_ + 8], lhsT=s0b[:, s_, :], rhs=qdec[:, 8 * s_:8 * s_ + 8])
                        ps_o2 = bank(128)
                        P.mm(ps_o2, lhsT=u16, rhs=qkT)
                        P.copy("act", tl(osb, t), ps_o)
                        P.tt("dve", tl(osb, t), tl(osb, t), ps_o2, ALU.add)
                        kdm = V(big16.ap[:, 10240:12288].rearrange("p (s v) -> p s v", s=16), ("bg_kdm",))
                        P.tt("dve", kdm, kdec.un(1).bc([128, 16, 128]),
                             cv("segmask").un(2).bc([128, 16, 128]), ALU.mult)
                        snew = V(big32.ap[:, 2048:4096].rearrange("p (s v) -> p s v", s=16), ("bg_snew",))
                        for b in range(4):
                            ps = bank()
                            for s4 in range(4):
                                s_ = 4 * b + s4
                                P.mm(ps[:, s4 * 128:(s4 + 1) * 128], lhsT=kdm[:, s_, :], rhs=u16)
                            P.tt("dve", snew[:, 4 * b:4 * b + 4, :], s0f[:, 4 * b:4 * b + 4, :],
                                 gls[:, 4 * b:4 * b + 4].un(2).bc([128, 4, 128]), ALU.mult)
                            P.tt("dve", snew[:, 4 * b:4 * b + 4, :], snew[:, 4 * b:4 * b + 4, :],
                                 ps.re("p (s v) -> p s v", s=4), ALU.add)
                        P.dma(cfg.get("oq", "act"), [(sd_s[:, hd].rearrange("s d v -> d s v"), snew.ap)], "snew", reads=snew.keys)
                    yield
                sq2 = wt("sq2c", 2, (128, 512), BF16)
                P.act(sq2[:, 0:TW], osb[:, 0:TW], AF.Square)
                ps = bank(TW)
                P.mm(ps, lhsT=ones128, rhs=sq2[:, 0:TW])
                ro = t512.next()
                P.act(ro[:, 0:TW], ps, AF.Ln, bias=EPS)
                P.act(ro[:, 0:TW], ro[:, 0:TW], AF.Exp, scale=-0.5)
                P.stt("dve", osb[:, 0:TW], osb[:, 0:TW], gains[:, 64:65], ro[:, 0:TW], ALU.mult, ALU.mult)
                P.tt("dve", og[:, hd, 0:TW], osb[:, 0:TW], W["zs"][:, 0:TW], ALU.mult)
                yield

            def drive(gen, box=None):
                try:
                    next(gen)
                    return True
                except StopIteration as e:
                    if box is not None:
                        box.append(e.value)
                    return False

            chains = {}

            def step_chains():
                for k_ in sorted(chains):
                    if not drive(chains[k_]):
                        del chains[k_]
            pending = None
            for hd in range(NH + 1):
                while hd < NH and ((hd - 3) in chains or len(chains) > 2):
                    step_chains()
                pg_ = setup(hd) if hd < NH else None
                pbox = []
                hg_ = halves(*pending) if pending is not None else None
                hbox = []
                while pg_ is not None or hg_ is not None:
                    step_chains()
                    if hg_ is not None and not drive(hg_, hbox):
                        hg_ = None
                    if pg_ is not None and not drive(pg_, pbox):
                        pg_ = None
                if pending is not None:
                    chains[pending[0]] = chain(pending[0], hbox[0])
                pending = pbox[0] if hd < NH else None
            while chains:
                step_chains()
            if not sample and last_g:
                P.dma(cfg.get("oq", "act"), [(sd_p[seq].rearrange("h d v -> d h v"), S32.ap)], "S32o", reads=S32.keys)
                stg = xst.next()
                for b in range(6):
                    ps = mmr.next()
                    for c4 in range(4):
                        ch = b * 4 + c4
                        P.mm(ps[0:3, c4 * 128:(c4 + 1) * 128], lhsT=cc[:, ch, :], rhs=identF)
                    P.copy("act", stg[0:3, (b % 2) * 512:(b % 2) * 512 + 512], ps[0:3, :])
                    if b % 2 == 1:
                        pc = b // 2
                        P.dma(cfg.get("oq", "act"), [(sc_p[seq * 3:seq * 3 + 3, pc * 1024:(pc + 1) * 1024], stg.ap[0:3, :])],
                              stg.keys[0] + "o", reads=stg.keys)
                        if b < 5:
                            stg = xst.next()
            if sample:
                stg = xst.next()
                for b in range(6):
                    ps = mmr.next()
                    for c4 in range(4):
                        ch = b * 4 + c4
                        P.mm(ps[0:48, c4 * 128:(c4 + 1) * 128], lhsT=ccs[:, ch, :, :].re("p q w -> p (q w)"),
                             rhs=identF)
                    P.copy("act", stg[0:48, (b % 2) * 512:(b % 2) * 512 + 512], ps[0:48, :])
                    if b % 2 == 1:
                        pc = b // 2
                        P.dma(cfg.get("oq", "act"), [(sc_s[:, pc * 1024:(pc + 1) * 1024], stg.ap[0:48, :])], stg.keys[0] + "o",
                              reads=stg.keys)
                        if b < 5:
                            stg = xst.next()
            for half in range(2):
                slot = wload(f"out{half}")
                for mi in range(4):
                    m = half * 4 + mi
                    ps = mmr.next()
                    for c in range(8):
                        P.mm(ps[:, :N], lhsT=slot[:, c * 512 + mi * 128:c * 512 + mi * 128 + 128], rhs=og[:, c, :N],
                             start=(c == 0), stop=(c == 7))
                    P.tt("dve", h[:, m, :N], h[:, m, :N], ps[:, :N], ALU.add)

        kall = sb("kall", [128, 4, 640], BF16)
        vtok = sb("vtok", [128, 5, 4, 2, 64], BF16)
        qall = sb("qall", [128, 8, 512], BF16)
        smb = sb("smb", [128, 2176])
        pmt = es.enter_context(nc.sbuf_tensor("Pm", [128, 1152], F32))
        Pm32 = V(pmt[:], ("Pm",))
        Pm = V(pmt[:].bitcast(BF16), ("Pm",))
        PT = sb("PT", [128, 17, 128], BF16)
        if SAMPLE:
            kck = V(big16.ap[:, 4352:6400], ("bg_kck",))
            ckb = V(big16.ap[:, 6400:8448].rearrange("p (s o d) -> p s o d", s=16, o=2), ("bg_ckb",))
            vcb = V(big16.ap[:, 8448:10496].rearrange("p (s o d) -> p s o d", s=16, o=2), ("bg_vcb",))

        def layer_b(kind, seq, N, nt, first, last_g):
            sample = kind == "s"

            def stop(k):
                if cfg.get("stop") == k:
                    raise StopBuild()
            rms(N)
            normed(xn, GI["kv"], N)
            slot = wload("kv")
            kslot = wload("kdup")
            if first and not sample:
                P.memset("dve", kall[:, :, 0:128], 0.0)
                P.memset("dve", vtok[:, 0], 0.0)
            for kvh in range(4):
                ps = mmr.next()
                for c in range(8):
                    P.mm(ps[:, :N], lhsT=kslot[:, c * 512 + kvh * 128:c * 512 + kvh * 128 + 128], rhs=xn[:, c, :N],
                         start=(c == 0), stop=(c == 7))
                P.copy("act", kall[:, kvh, 128:128 + N], ps[:, :N])
            stop(41)
            for t in range(nt):
                ps = mmr.next()
                for c in range(8):
                    P.mm(ps, lhsT=xn[:, c, t * 128:(t + 1) * 128], rhs=slot[:, c * 512:(c + 1) * 512],
                         start=(c == 0), stop=(c == 7))
                for o in range(2):
                    if cfg.get("skip_vtok"):
                        continue
                    P.copy("dve", vtok[:, t + 1, :, o, :], ps[:, 256:512].re("p (k d) -> p k d", k=4))
                if (sample or (last_g and t == nt - 1)) and not cfg.get("skip_kvt"):
                    kvt = t512.next()
                    P.copy("act", kvt, ps)
                    if sample:
                        pairs = []
                        for sq_ in range(16):
                            pairs.append((kw_s[sq_, 120:128, :], kvt.ap[sq_ * 8:(sq_ + 1) * 8, 0:256]))
                            pairs.append((vw_s[sq_, 120:128, :], kvt.ap[sq_ * 8:(sq_ + 1) * 8, 256:512]))
                        P.dma(cfg.get("oq", "act"), pairs, kvt.keys[0] + "o", reads=kvt.keys)
                    else:
                        if cfg.get("kvt_nodma"):
                            pass
                        elif cfg.get("kvt_one"):
                            P.dma("sp", [(kw_p[seq], kvt.ap[:, 0:256])], kvt.keys[0] + "o", reads=kvt.keys)
                        elif cfg.get("kvt_two"):
                            P.dma("sp", [(kw_p[seq], kvt.ap[:, 0:256])], kvt.keys[0] + "o", reads=kvt.keys)
                            P.dma("sp", [(vw_p[seq], kvt.ap[:, 256:512])], kvt.keys[0] + "o2", reads=kvt.keys)
                        else:
                            P.dma(cfg.get("oq", "act"), [(kw_p[seq], kvt.ap[:, 0:256]), (vw_p[seq], kvt.ap[:, 256:512])],
                                  kvt.keys[0] + "o", reads=kvt.keys)
            stop(42)
            normed(xn, GI["b"], N)
            for half in range(2):
                slot = wload(f"q{half}")
                for mi in range(4):
                    m = half * 4 + mi
                    ps = mmr.next()
                    for c in range(8):
                        P.mm(ps[:, :N], lhsT=slot[:, c * 512 + mi * 128:c * 512 + mi * 128 + 128], rhs=xn[:, c, :N],
                             start=(c == 0), stop=(c == 7))
                    P.copy("act", qall[:, m, :N], ps[:, :N], scale=0.125)
            stop(43)
            if sample:
                smask = V(big32.ap[:, 0:2176], ("bg_smask",))
                P.dma("sp", [(smask.ap, smask_d)], "smaskd", writes=smask.keys)
                P.dma("sp", [(kw_s[:, 0:120, :], ck_in[:, 8:128, :]), (vw_s[:, 0:120, :], cv_in[:, 8:128, :])], "cpass")
            def attn_unit(t, kvh, par):
                tc_ = slice(t * 128, (t + 1) * 128)
                hq0 = 4 * kvh
                smb_p = V(smb.ap[:, par * 1024:(par + 1) * 1024], (f"smb_{par}",))
                Pm_p = V(Pm.ap[:, par * 1024:(par + 1) * 1024], (f"Pm_{par}",))
                PT_p = V(PT.ap[:, par * 8:(par + 1) * 8, :], (f"PT_{par}",))
                mk = cv("amask0") if (first and t == 0) else cv("amask")
                vbl = [vtok[:, t, kvh], vtok[:, t + 1, kvh]]
                gmap = [0, 2, 1, 3]
                for b2 in range(2):
                    ps = bank()
                    for gi in range(2):
                        g = gmap[b2 * 2 + gi]
                        ph = slice((g % 2) * 64, (g % 2) * 64 + 64)
                        P.mm(ps[:, gi * 256:(gi + 1) * 256], lhsT=qall[ph, (hq0 + g) // 2, tc_],
                             rhs=kall[ph, kvh, t * 128:t * 128 + 256])
                    P.tt("dve", smb_p[:, b2 * 512:(b2 + 1) * 512].re("p (g k) -> p g k", g=2),
                         ps.re("p (g k) -> p g k", g=2), mk.un(1).bc([128, 2, 256]), ALU.add)
                stop(51)
                yield
                mx4 = wt("mx4", 2, (128, 4))
                P.red("dve", mx4, smb_p.re("p (g k) -> p g k", g=4), ALU.max)
                nmx4 = wt("nmx4", 2, (128, 4))
                P.stt("dve", nmx4, mx4, -1.0, negsink[:, hq0:hq0 + 4], ALU.mult, ALU.min)
                rs4 = wt("rs4", 2, (128, 4))
                for g in range(4):
                    P.act(Pm_p[:, g * 256:(g + 1) * 256], smb_p[:, g * 256:(g + 1) * 256], AF.Exp,
                          bias=nmx4[:, g:g + 1], accum=rs4[:, g:g + 1])
                stop(52)
                yield
                es4 = wt("es4", 2, (128, 4))
                P.tt("dve", es4, sinkp[:, hq0:hq0 + 4], nmx4, ALU.add)
                P.act(es4, es4, AF.Exp)
                P.tt("dve", es4, es4, rs4, ALU.add)
                rl4 = wt("rl4", 2, (128, 4))
                P.recip(rl4, es4)
                dg4 = wt("dg4", 2, (128, 4, 128), BF16)
                P.tt("dve", dg4, identB.un(1).bc([128, 4, 128]), rl4.un(2).bc([128, 4, 128]), ALU.mult)
                stop(53)
                yield
                for b2 in range(2):
                    ps = bank()
                    for gi in range(2):
                        g = b2 * 2 + gi
                        for kb in range(2):
                            j = gi * 2 + kb
                            P.mm(ps[:, j * 128:(j + 1) * 128],
                                 lhsT=Pm_p[:, g * 256 + kb * 128:g * 256 + (kb + 1) * 128], rhs=dg4[:, g, :])
                    P.copy("act" if b2 == 0 else "dve", PT_p[:, b2 * 4:(b2 + 1) * 4, :],
                           ps.re("p (b n) -> p b n", b=4))
                stop(54)
                yield
                pso = bank()
                for g in range(4):
                    for kb in range(2):
                        P.mm(pso[:, g * 128:(g + 1) * 128], lhsT=vbl[kb].re("p o d -> p (o d)"),
                             rhs=PT_p[:, g * 2 + kb, :], start=(kb == 0), stop=(kb == 1))
                for sl in range(4):
                    g = gmap[sl]
                    ph = slice((g % 2) * 64, (g % 2) * 64 + 64)
                    P.copy("act" if sl % 2 == 0 else "dve", og[ph, (hq0 + g) // 2, tc_], pso[ph, sl * 128:(sl + 1) * 128])
                yield

            if not sample:
                units = [attn_unit(t, kvh, (t * 4 + kvh) % 2) for t in range(nt) for kvh in range(4)]
                active = []
                while units or active:
                    while units and len(active) < cfg.get("aw", 2):
                        active.append([units.pop(0), 0])
                    for a_ in list(active):
                        try:
                            next(a_[0])
                            a_[1] += 1
                        except StopIteration:
                            active.remove(a_)
            for t in range(nt if sample else 0):
                tc_ = slice(t * 128, (t + 1) * 128)
                for hq in range(16):
                    kvh = hq // 4
                    hf = hq % 2
                    ph = slice(hf * 64, (hf + 1) * 64)
                    if sample and hq % 4 == 0:
                        pk, pv = [], []
                        for o in range(2):
                            pk.append((ckb.ap[:, :, o, :], ck_in[:, :, kvh * 64:(kvh + 1) * 64].rearrange("s n d -> n s d")))
                            pv.append((vcb.ap[:, :, o, :], cv_in[:, :, kvh * 64:(kvh + 1) * 64].rearrange("s n d -> n s d")))
                        P.dma("pool", pk, "ckb", writes=ckb.keys)
                        P.dma("pool", pv, "vcb", writes=vcb.keys)
                        for s4 in range(4):
                            ps = mmr.next()
                            for si in range(4):
                                s_ = s4 * 4 + si
                                P.mm(ps[:, si * 128:(si + 1) * 128], lhsT=ckb[:, s_].re("p o d -> p (o d)"), rhs=identB)
                            P.copy("act", kck[:, s4 * 512:(s4 + 1) * 512], ps)
                    if sample:
                        pieces = [(kck[ph, p * 512:(p + 1) * 512], smask[:, p * 512:(p + 1) * 512], 512)
                                  for p in range(4)]
                        pieces.append((kall[ph, kvh, 128:256], smask[:, 2048:2176], 128))
                        vbl = [vcb[:, s_] for s_ in range(16)] + [vtok[:, 1, kvh]]
                    else:
                        mk = cv("amask0") if (first and t == 0) else cv("amask")
                        pieces = [(kall[ph, kvh, t * 128:t * 128 + 256], mk, 256)]
                        vbl = [vtok[:, t, kvh], vtok[:, t + 1, kvh]]
                    off = 0
                    for (kk, mk, w) in pieces:
                        ps = mmr.next()
                        P.mm(ps[:, :w], lhsT=qall[ph, hq // 2, tc_], rhs=kk)
                        P.tt("dve", smb[:, off:off + w], ps[:, :w], mk, ALU.add)
                        off += w
                    nk = off
                    stop(44)
                    mx = wt("mx", 2, (128, 1))
                    P.red("dve", mx, smb[:, :nk], ALU.max)
                    nmx = wt("nmx", 2, (128, 1))
                    P.ts("dve", nmx, mx, v8[:, 16 + hq:17 + hq], kc[:, 1:2], ALU.max, ALU.mult)
                    rsum = wt("rsum", 2, (128, 1))
                    P.act(Pm[:, :nk], smb[:, :nk], AF.Exp, bias=nmx, accum=rsum)
                    esk = wt("esk", 2, (128, 1))
                    P.act(esk, v8[:, 16 + hq:17 + hq], AF.Exp, bias=nmx)
                    P.tt("dve", esk, esk, rsum, ALU.add)
                    rl = wt("rl", 2, (128, 1))
                    P.recip(rl, esk)
                    stop(45)
                    dg = wt("dg", 2, (128, 128), BF16)
                    P.ts("dve", dg, identB, rl, None, ALU.mult)
                    nkb = nk // 128
                    for b0 in range(0, nkb, 4):
                        nb = min(4, nkb - b0)
                        ps = mmr.next()
                        for bi in range(nb):
                            kb = b0 + bi
                            P.mm(ps[:, bi * 128:(bi + 1) * 128], lhsT=Pm[:, kb * 128:(kb + 1) * 128], rhs=dg)
                        P.copy("act", PT[:, b0:b0 + nb, :], ps[:, :nb * 128].re("p (b n) -> p b n", b=nb))
                    pso = smr.next()
                    for kb in range(nkb):
                        P.mm(pso, lhsT=vbl[kb].re("p o d -> p (o d)"), rhs=PT[:, kb, :], start=(kb == 0),
                             stop=(kb == nkb - 1))
                    P.copy("act", og[ph, hq // 2, tc_], pso[ph, :])
                    stop(46)
                    if hq == 1:
                        stop(47)
            if not sample:
                P.copy("act", kall[:, :, 0:128], kall[:, :, N:N + 128])
                P.copy("act", vtok[:, 0], vtok[:, nt])
            for half in range(2):
                slot = wload(f"o{half}")
                for mi in range(4):
                    m = half * 4 + mi
                    ps = mmr.next()
                    for c in range(8):
                        P.mm(ps[:, :N], lhsT=slot[:, c * 512 + mi * 128:c * 512 + mi * 128 + 128], rhs=og[:, c, :N],
                             start=(c == 0), stop=(c == 7))
                    P.tt("dve", h[:, m, :N], h[:, m, :N], ps[:, :N], ALU.add)

        def group(kind, seq, g, nt, first, last_g):
            N = nt * 128
            if kind == "p":
                r0 = (seq * TPS + g * 4) * 128
                xd, pd, yd = x_p, pe_p, y_p
            else:
                r0 = 0
                xd, pd, yd = x_s, pe_s, y_s
            for t in range(nt):
                rows = slice(r0 + t * 128, r0 + (t + 1) * 128)
                xs = xst.next()
                P.dma("sp", [(xs.ap, xd[rows, :])], xs.keys[0], writes=xs.keys)
                for half in range(2):
                    ps = mmr.next()
                    for c4 in range(4):
                        c = half * 4 + c4
                        P.mm(ps[:, c4 * 128:(c4 + 1) * 128], lhsT=xs[:, c * 128:(c + 1) * 128], rhs=identF)
                    P.copy("act", h[:, half * 4:(half + 1) * 4, t * 128:(t + 1) * 128],
                           ps.re("p (c n) -> p c n", c=4))
                for l in range(2):
                    pt = pst.next()
                    P.dma("sp", [(pt.ap, pd[l, rows, :])], pt.keys[0], writes=pt.keys)
                    ps = mmr.next()
                    for c in range(2):
                        P.mm(ps[:, c * 128:(c + 1) * 128], lhsT=pt[:, c * 128:(c + 1) * 128], rhs=identF)
                    P.copy("act", peT[l][:, :, t * 128:(t + 1) * 128], ps[:, 0:256].re("p (c n) -> p c n", c=2))
            def stop(k):
                if cfg.get("stop") == k:
                    raise StopBuild()
            stop(1)
            if kind == "p" and first:
                P.memset("dve", S32, 0.0)
                P.memset("dve", S16, 0.0)
            BGK = ["big", "bg_s0f", "bg_s0b", "bg_kdm", "bg_snew"] + [f"g_{n_}_{i_}" for i_ in range(2) for n_ in
                                                                       ["LmP", "UmP", "L3", "U3", "PLa", "PLb"]]
            if kind == "s":
                P.op("dve", lambda e: e.memset(kc.ap[:, 6:7], 0.0), reads=(), writes=tuple(BGK + ["kcy"]))
            layer_a(kind, seq, N, nt, first, last_g)
            if kind == "s":
                P.op("dve", lambda e: e.memset(kc.ap[:, 6:7], 0.0), reads=(), writes=tuple(BGK + ["kcy"]))
            stop(2)
            ffn(0, N)
            stop(3)
            ple(0, N)
            stop(4)
            BGB = ["big", "bg_smask", "bg_kck", "bg_ckb", "bg_vcb"]
            if kind == "s":
                P.op("dve", lambda e: e.memset(kc.ap[:, 5:6], 0.0), reads=(), writes=tuple(BGB + ["kcz"]))
            layer_b(kind, seq, N, nt, first, last_g)
            if kind == "s":
                P.op("dve", lambda e: e.memset(kc.ap[:, 5:6], 0.0), reads=(), writes=tuple(BGB + ["kcz"]))
            stop(5)
            ffn(1, N)
            ple(1, N)
            stop(6)
            rms(N)
            for c in range(8):
                P.stt("dve", h[:, c, :N], h[:, c, :N], gains[:, GI["fin"] * 8 + c:GI["fin"] * 8 + c + 1], rs[:, :N],
                      ALU.mult, ALU.mult)
            for t in range(nt):
                rows = slice(r0 + t * 128, r0 + (t + 1) * 128)
                ys = xst.next()
                for half in range(2):
                    ps = mmr.next()
                    for c4 in range(4):
                        c = half * 4 + c4
                        P.mm(ps[:, c4 * 128:(c4 + 1) * 128], lhsT=h[:, c, t * 128:(t + 1) * 128], rhs=identF)
                    P.copy("act", ys[:, half * 512:(half + 1) * 512], ps)
                P.dma(cfg.get("oq", "act"), [(yd[rows, :], ys.ap)], ys.keys[0] + "o", reads=ys.keys)

        def simple(w2d, c0, ncols, kcn):
            return [(lambda ap, n=ncols, k=kcn: part_dst(ap, 0, k * n, k), wsrc(w2d, c0, ncols), kcn * ncols)]
        for hd in range(NH):
            wdef(f"in{hd}", [(lambda ap, s_=s_: part_dst(ap, s_ * 1024, (s_ + 1) * 1024, 8),
                              wsrc(w_in, s_ * 1024 + hd * 128, 128), (s_ + 1) * 1024) for s_ in range(4)])
        for half in range(2):
            wdef(f"out{half}", simple(w_out, half * 512, 512, 8))
        for l in range(2):
            for jj in range(NFC // 2):
                wdef(f"gu{l}_{jj}", [(lambda ap: part_dst(ap, 0, 2048, 8), wsrc(w_gu[l], jj * 256, 256), 2048),
                                     (lambda ap: part_dst(ap, 2048, 4096, 8), wsrc(w_gu[l], FF + jj * 256, 256), 4096)])
            for m in range(8):
                wdef(f"dn{l}_{m}", simple(w_down[l], m * 128, 128, NFC))
            wdef(f"pp{l}", simple(w_pp[l], 0, 1024, 2))
            for half in range(2):
                wdef(f"pg{l}_{half}", simple(w_pg[l], half * 512, 512, 8))
        wdef("kv", simple(w_kv, 0, 512, 8))
        wdef("kdup", [(lambda ap, kvh=kvh, o=o: ap.rearrange("p (c k o d) -> p c k o d", c=8, k=4, o=2)[:, :, kvh, o, :],
                       wsrc(w_kv, kvh * 64, 64), 4096) for kvh in range(4) for o in range(2)])
        for half in range(2):
            wdef(f"q{half}", simple(w_q, half * 512, 512, 8))
        for half in range(2):
            wdef(f"o{half}", simple(w_o, half * 512, 512, 8))
        NT = len(WT)
        wsc = nc.dram_tensor("wsc", [NT, 128, 4096], BF16, kind="Internal").ap()
        stf = Ring([h.re("p c n -> p (c n)"), big32[:, 0:4096]])
        stb = Ring([xn.re("p c n -> p (c n)"), og.re("p c n -> p (c n)"), qall.re("p c n -> p (c n)")])
        cast_eng = ["dve", "act", "pool"]
        wlist = list(WT.items())
        f32s = {}

        def pre_in(i):
            name, (ti, parts, n_used) = wlist[i]
            f32t = stf.next()
            f32s[i] = f32t
            P.dma("sp", [(dfn(f32t.ap), src) for (dfn, src, _) in parts], f32t.keys[0] + "w", writes=f32t.keys)
        for i in range(min(2, len(wlist))):
            pre_in(i)
        for i, (name, (ti, parts, n_used)) in enumerate(wlist):
            f32t = f32s.pop(i)
            b16t = stb.next()
            P.copy(cast_eng[ti % 3], b16t[:, 0:n_used], f32t[:, 0:n_used])
            if i + 2 < len(wlist):
                pre_in(i + 2)
            P.dma("act", [(wsc[ti][:, 0:n_used], b16t.ap[:, 0:n_used])], b16t.keys[0] + "w", reads=b16t.keys,
                  writes=(f"wsc{ti}",))
        P.op("dve", lambda e: e.memset(kc.ap[:, 7:8], 0.0), reads=(),
             writes=tuple(["qall", "big", "kcx"] + [f"g_{n}_{i}" for i in range(2) for n in
                           ["D", "dL", "E", "Z", "Z2", "LmP", "UmP", "L3", "U3", "PLa", "PLb"]]))
        try:
            if cfg.get("stop") == 0:
                raise StopBuild()
            for seq in range(NSEQ):
                ng = TPS // 4
                for g in range(ng):
                    group("p", seq, g, 4, g == 0, g == ng - 1)
            if SAMPLE:
                group("s", 0, 0, 1, True, True)
        except StopBuild:
            pass
        P.emit()
        cfg["nops"] = P.nops
    return nc, (cpack, smask_np)


_CACHE = {}


def host_inputs(inp, core, cfg, cpack):
    NSEQ, TPS = cfg["nseq"], cfg["tps"]
    L = TPS * 128
    f = np.float32
    m = {}
    s0 = core * NSEQ
    if NSEQ:
        m["x_p"] = np.ascontiguousarray(inp["x_prompt"][s0:s0 + NSEQ, :L].reshape(NSEQ * L, D), f)
        m["pe_p"] = np.ascontiguousarray(inp["p_prompt"][:, s0:s0 + NSEQ, :L].reshape(2, NSEQ * L, 256), f)
    else:
        m["x_p"] = np.zeros((1, D), f)
        m["pe_p"] = np.zeros((2, 1, 256), f)
    b0 = core * 16
    m["x_s"] = np.ascontiguousarray(inp["x_sample"][b0:b0 + 16].reshape(128, D), f)
    m["pe_s"] = np.ascontiguousarray(inp["p_sample"][:, b0:b0 + 16].reshape(2, 128, 256), f)
    m["sd_in"] = np.ascontiguousarray(inp["state_delta"][0, b0:b0 + 16], f)
    m["sc_in"] = np.ascontiguousarray(inp["state_conv"][0, b0:b0 + 16].reshape(48, 3072), f)
    m["ck_in"] = np.ascontiguousarray(inp["cache_k_win"][b0:b0 + 16].reshape(16, 128, 256), f)
    m["cv_in"] = np.ascontiguousarray(inp["cache_v_win"][b0:b0 + 16].reshape(16, 128, 256), f)
    return m


def shared_inputs(inp, cpack):
    f = np.float32
    m = {}
    m["w_in"] = np.ascontiguousarray(inp["a_w_in"][0], f)
    m["w_out"] = np.ascontiguousarray(inp["a_w_out"][0], f)
    m["w_gu"] = np.ascontiguousarray(inp["ffn_w_gu"], f)
    m["w_down"] = np.ascontiguousarray(inp["ffn_w_down"], f)
    m["w_pp"] = np.ascontiguousarray(inp["ple_w_proj"], f)
    m["w_pg"] = np.ascontiguousarray(inp["ple_w_gate"], f)
    m["w_kv"] = np.ascontiguousarray(inp["kv_w"], f)
    m["w_q"] = np.ascontiguousarray(inp["b_w_q"][0], f)
    m["w_o"] = np.ascontiguousarray(inp["b_w_o"][0], f)
    gl = [inp["a_norm"][0], inp["ffn_norm"][0], inp["ffn_norm"][1], inp["ple_norm"][0], inp["ple_norm"][1],
          inp["kv_norm"], inp["b_norm"][0], inp["final_norm"]]
    pa = np.concatenate([np.asarray(g, f).reshape(8, 128) for g in gl] + [np.asarray(inp["a_out_norm"][0], f).reshape(1, 128)], 0)
    m["packA"] = np.ascontiguousarray(pa, f)
    m["packB"] = np.ascontiguousarray(np.asarray(inp["a_conv_w"][0], f).reshape(4 * 24, 128), f)
    m["vec8"] = np.ascontiguousarray(np.concatenate([np.asarray(inp["a_a_log"][0], f), np.asarray(inp["a_dt_bias"][0], f),
                                                     np.asarray(inp["b_sinks"][0], f)]).reshape(1, 32), f)
    m["cst"] = cpack[0]
    m["smask"] = cpack[1]
    return m


def run(inp, cfg, ncores):
    key = (cfg["nseq"], cfg["tps"], cfg["sample"])
    if key not in _CACHE:
        _CACHE[key] = build(cfg)
    nc, cpack = _CACHE[key]
    shared = shared_inputs(inp, cpack)
    in_maps = []
    for c in range(ncores):
        m = dict(shared)
        m.update(host_inputs(inp, c, cfg, cpack))
        in_maps.append(m)
    res = run_bass_kernel_spmd(nc, in_maps, core_ids=list(range(ncores)))
    return res.results


def kernel(**inputs):
    inp = {k: np.asarray(v) for k, v in inputs.items()}
    cfg = {"nseq": 2, "tps": 16, "sample": True}
    r = run(inp, cfg, 8)
    f = np.float32
    y_p = np.concatenate([x["y_p"].reshape(2, SEQ, D) for x in r], 0).astype(f)
    y_s = np.concatenate([x["y_s"].reshape(16, 8, D) for x in r], 0).astype(f)
    sd_p = np.concatenate([x["sd_p"] for x in r], 0)[None].astype(f)
    sd_s = np.concatenate([x["sd_s"] for x in r], 0)[None].astype(f)
    sc_p = np.concatenate([x["sc_p"].reshape(2, 3, 3072) for x in r], 0)[None].astype(f)
    sc_s = np.concatenate([x["sc_s"].reshape(16, 3, 3072) for x in r], 0)[None].astype(f)
    kw_p = np.concatenate([x["kw_p"].reshape(2, 128, 4, 64) for x in r], 0).astype(f)
    kw_s = np.concatenate([x["kw_s"].reshape(16, 128, 4, 64) for x in r], 0).astype(f)
    vw_p = np.concatenate([x["vw_p"].reshape(2, 128, 4, 64) for x in r], 0).astype(f)
    vw_s = np.concatenate([x["vw_s"].reshape(16, 128, 4, 64) for x in r], 0).astype(f)
    return (y_p, y_s, sd_p, sd_s, sc_p, sc_s, kw_p, kw_s, vw_p, vw_s)
```

```python
import contextlib
import numpy as np
import concourse.bass as bass
import concourse.mybir as mybir
from concourse.bass_utils import run_bass_kernel_spmd

F32 = mybir.dt.float32
BF16 = mybir.dt.bfloat16
AF = mybir.ActivationFunctionType
ALU = mybir.AluOpType
AX = mybir.AxisListType

D = 1024
SEQ = 2048
NH = 8
FF = 2816
NFC = 22
EPS = 1e-6
NEG = -1e30
SEM_LIM = 30000


class StopBuild(Exception):
    pass


class V:
    __slots__ = ("ap", "keys")

    def __init__(self, ap, keys):
        self.ap = ap
        self.keys = tuple(keys)

    def __getitem__(self, idx):
        return V(self.ap[idx], self.keys)

    def bc(self, shape):
        return V(self.ap.to_broadcast(list(shape)), self.keys)

    def un(self, axis):
        return V(self.ap.unsqueeze(axis), self.keys)

    def re(self, pat, **kw):
        return V(self.ap.rearrange(pat, **kw), self.keys)


class VC(V):
    __slots__ = ("name", "n")

    def __init__(self, ap, name, n):
        V.__init__(self, ap, tuple(f"{name}{i}" for i in range(n)))
        self.name = name
        self.n = n

    def __getitem__(self, idx):
        keys = self.keys
        if isinstance(idx, tuple) and len(idx) >= 2:
            i1 = idx[1]
            if isinstance(i1, int):
                keys = (f"{self.name}{i1}",)
            elif isinstance(i1, slice):
                keys = tuple(f"{self.name}{i}" for i in range(*i1.indices(self.n)))
        return V(self.ap[idx], keys)


class Op:
    __slots__ = ("eng", "fn", "deps", "signaled", "is_dma", "sem", "val", "idx")

    def __init__(self, eng, fn):
        self.eng = eng
        self.fn = fn
        self.deps = []
        self.signaled = False
        self.is_dma = False
        self.sem = None
        self.val = 0


class Ring:
    def __init__(self, views):
        self.views = views
        self.i = 0

    def next(self):
        v = self.views[self.i % len(self.views)]
        self.i += 1
        return v


class Prog:
    ENGS = ("pe", "act", "dve", "pool", "sp")

    def __init__(self, nc, es):
        self.nc = nc
        self.es = es
        self.ops = {e: [] for e in self.ENGS}
        self.lastw = {}
        self.readers = {}
        self.dma_sems = {}
        self.dma_cnt = {}
        self.dma_last = {}
        self.nops = 0
        self.psum_keys = set()

    def op(self, eng, fn, reads=(), writes=()):
        o = Op(eng, fn)
        deps = {}
        for k in reads:
            w = self.lastw.get(k)
            if w is not None:
                deps[id(w)] = w
            if k in self.psum_keys:
                for r in self.readers.get(k, ()):
                    if r.eng != eng:
                        deps[id(r)] = r
        for k in writes:
            w = self.lastw.get(k)
            if w is not None:
                deps[id(w)] = w
            for r in self.readers.get(k, ()):
                deps[id(r)] = r
        for d in deps.values():
            if d.eng == "pe" and eng == "pe" and not d.is_dma:
                continue
            o.deps.append(d)
            d.signaled = True
        for k in reads:
            self.readers.setdefault(k, []).append(o)
        for k in writes:
            self.lastw[k] = o
            self.readers[k] = []
        self.ops[eng].append(o)
        self.nops += 1
        return o

    def dma(self, queue, pairs, key, reads=(), writes=()):
        if key not in self.dma_sems:
            self.dma_sems[key] = self.es.enter_context(self.nc.semaphore("d_" + key))
            self.dma_cnt[key] = 0
        sem = self.dma_sems[key]
        deps = {}
        for k in reads:
            w = self.lastw.get(k)
            if w is not None:
                deps[id(w)] = w
        for k in writes:
            w = self.lastw.get(k)
            if w is not None:
                deps[id(w)] = w
            for r in self.readers.get(k, ()):
                deps[id(r)] = r
        prev = self.dma_last.get(key)
        if prev is not None:
            deps[id(prev)] = prev
        for d in deps.values():
            d.signaled = True
        self.dma_cnt[key] += 16 * len(pairs)
        val = self.dma_cnt[key]
        last = None
        for i, (o_ap, i_ap) in enumerate(pairs):
            def fn(e, o_ap=o_ap, i_ap=i_ap):
                return e.dma_start(out=o_ap, in_=i_ap)
            o = Op(queue, fn)
            o.deps = list(deps.values()) if i == 0 else []
            o.is_dma = True
            o.sem = sem
            o.val = val
            self.ops[queue].append(o)
            self.nops += 1
            last = o
        for k in reads:
            self.readers.setdefault(k, []).append(last)
        for k in writes:
            self.lastw[k] = last
            self.readers[k] = []
        self.dma_last[key] = last
        return last

    def mm(self, out, lhsT, rhs, start=True, stop=True):
        self.op("pe", lambda e: e.matmul(out.ap, lhsT=lhsT.ap, rhs=rhs.ap, start=start, stop=stop),
                reads=lhsT.keys + rhs.keys, writes=out.keys)

    def act(self, out, in_, func, bias=None, scale=None, accum=None):
        kw = {}
        reads = list(in_.keys)
        writes = list(out.keys)
        if bias is not None:
            if isinstance(bias, V):
                kw["bias"] = bias.ap
                reads += bias.keys
            else:
                kw["bias"] = float(bias)
        if scale is not None:
            if isinstance(scale, V):
                kw["scale"] = scale.ap
                reads += scale.keys
            else:
                kw["scale"] = float(scale)
        if accum is not None:
            kw["accum_out"] = accum.ap
            writes += accum.keys
        self.op("act", lambda e: e.activation(out=out.ap, in_=in_.ap, func=func, **kw), reads, writes)

    def ts(self, eng, out, in0, s1, s2, op0, op1=None):
        reads = list(in0.keys)
        a1 = s1
        a2 = s2
        if isinstance(s1, V):
            a1 = s1.ap
            reads += s1.keys
        if isinstance(s2, V):
            a2 = s2.ap
            reads += s2.keys
        kw = {}
        if op1 is not None:
            kw["op1"] = op1

        def fn(e):
            return e.tensor_scalar(out=out.ap, in0=in0.ap, scalar1=a1, scalar2=a2, op0=op0, **kw)
        self.op(eng, fn, reads, out.keys)

    def stt(self, eng, out, in0, scalar, in1, op0, op1):
        reads = list(in0.keys) + list(in1.keys)
        a = scalar
        if isinstance(scalar, V):
            a = scalar.ap
            reads += scalar.keys
        self.op(eng, lambda e: e.scalar_tensor_tensor(out=out.ap, in0=in0.ap, scalar=a, in1=in1.ap, op0=op0, op1=op1),
                reads, out.keys)

    def tt(self, eng, out, in0, in1, op):
        self.op(eng, lambda e: e.tensor_tensor(out=out.ap, in0=in0.ap, in1=in1.ap, op=op),
                in0.keys + in1.keys, out.keys)

    def red(self, eng, out, in_, op):
        self.op(eng, lambda e: e.tensor_reduce(out=out.ap, in_=in_.ap, axis=AX.X, op=op), in_.keys, out.keys)

    def copy(self, eng, out, in_, scale=None):
        if eng == "act":
            self.act(out, in_, AF.Copy, scale=scale)
        else:
            assert scale is None
            self.op(eng, lambda e: e.tensor_copy(out=out.ap, in_=in_.ap), in_.keys, out.keys)

    def recip(self, out, in_):
        self.op("dve", lambda e: e.reciprocal(out=out.ap, in_=in_.ap), in_.keys, out.keys)

    def memset(self, eng, out, val):
        self.op(eng, lambda e: e.memset(out.ap, val), (), out.keys)

    def emit(self):
        nc, es = self.nc, self.es
        sems = {}
        for eng in ("pe", "act", "dve", "pool"):
            n = 0
            for o in self.ops[eng]:
                if o.is_dma:
                    continue
                if o.signaled:
                    o.idx = n
                    n += 1
            nsem = n // SEM_LIM + 1
            sems[eng] = [es.enter_context(nc.semaphore(f"s_{eng}{i}")) for i in range(nsem)]
            for o in self.ops[eng]:
                if not o.is_dma and o.signaled:
                    o.sem = sems[eng][o.idx // SEM_LIM]
                    o.val = o.idx % SEM_LIM + 1
        block = es.enter_context(nc.Block())
        final = [(self.dma_sems[k], self.dma_cnt[k]) for k in self.dma_sems]

        def run(name, e):
            waited = {}
            for o in self.ops[name]:
                for d in o.deps:
                    sid = id(d.sem)
                    if waited.get(sid, 0) < d.val:
                        e.wait_ge(d.sem, d.val)
                        waited[sid] = d.val
                ins = o.fn(e)
                if o.is_dma:
                    ins.then_inc(o.sem, 16)
                elif o.signaled:
                    ins.then_inc(o.sem, 1)
            if name == "sp":
                for (s, v) in final:
                    if waited.get(id(s), 0) < v:
                        e.wait_ge(s, v)

        @block.tensor
        def _(e):
            run("pe", e)

        @block.scalar
        def _(e):
            run("act", e)

        @block.vector
        def _(e):
            run("dve", e)

        @block.gpsimd
        def _(e):
            run("pool", e)

        @block.sync
        def _(e):
            run("sp", e)


def make_consts():
    i = np.arange(128)
    I, J = np.meshgrid(i, i, indexing="ij")
    c = {}
    c["ident"] = (I == J).astype(np.float32)
    c["tri_p"] = (I <= J).astype(np.float32)
    c["seg_p"] = np.ones((128, 128), np.float32)

    def lev(a, b):
        l = np.full(a.shape, 3)
        l[(a // 64) == (b // 64)] = 2
        l[(a // 32) == (b // 32)] = 1
        l[(a // 16) == (b // 16)] = 0
        return l
    L = lev(I, J)
    lower = I > J
    upper = I < J
    c["mL_p"] = np.stack([(lower & (L == k)).astype(np.float32) for k in range(4)], 1)
    c["mU_p"] = np.stack([(upper & (L == k)).astype(np.float32) for k in range(4)], 1)
    c["incU_p"] = (I <= J).astype(np.float32)
    same = (I // 8) == (J // 8)
    c["tri_s"] = ((I <= J) & same).astype(np.float32)
    c["seg_s"] = same.astype(np.float32)
    z = np.zeros((128, 128), np.float32)
    c["mL_s"] = (lower & same).astype(np.float32)
    c["mU_s"] = (upper & same).astype(np.float32)
    c["incU_s"] = ((I <= J) & same).astype(np.float32)
    sg = np.zeros((128, 16), np.float32)
    sg[i, i // 8] = 1.0
    c["segmask"] = sg
    q = np.arange(128)[:, None]
    k = np.arange(256)[None, :]
    rel = q + 128 - k
    ok = (rel >= 0) & (rel < 128)
    c["amask"] = np.where(ok, 0.0, NEG).astype(np.float32)
    ok0 = ok & (k >= 128)
    c["amask0"] = np.where(ok0, 0.0, NEG).astype(np.float32)
    s_q = (np.arange(128) // 8)[:, None]
    t_q = (np.arange(128) % 8)[:, None]
    s_k = (np.arange(2048) // 128)[None, :]
    n_k = (np.arange(2048) % 128)[None, :]
    okc = (s_k == s_q) & (n_k > t_q)
    s_n = (np.arange(128) // 8)[None, :]
    t_n = (np.arange(128) % 8)[None, :]
    okn = (s_n == s_q) & (t_n <= t_q)
    c["smask"] = np.where(np.concatenate([okc, okn], 1), 0.0, NEG).astype(np.float32)
    return c


CONST_ORDER = ["ident", "tri_p", "seg_p", "mL_p", "mU_p", "incU_p", "tri_s", "seg_s", "mL_s", "mU_s", "incU_s",
               "segmask", "amask", "amask0"]


def pack_consts():
    c = make_consts()
    offs = {}
    cols = []
    o = 0
    for k in CONST_ORDER:
        a = c[k].reshape(128, -1)
        offs[k] = (o, a.shape[1])
        cols.append(a)
        o += a.shape[1]
    return np.ascontiguousarray(np.concatenate(cols, 1)), offs, np.ascontiguousarray(c["smask"])


def build(cfg):
    NSEQ = cfg["nseq"]
    TPS = cfg["tps"]
    SAMPLE = cfg["sample"]
    NTOK = NSEQ * TPS * 128
    nc = bass.Bass("TRN2", target_bir_lowering=False)
    cpack, coffs, smask_np = pack_consts()
    NCC = cpack.shape[1]

    def din(name, shape):
        return nc.dram_tensor(name, list(shape), F32, kind="ExternalInput").ap()

    def dout(name, shape):
        return nc.dram_tensor(name, list(shape), F32, kind="ExternalOutput").ap()

    x_p = din("x_p", [max(NTOK, 1), D])
    pe_p = din("pe_p", [2, max(NTOK, 1), 256])
    x_s = din("x_s", [128, D])
    pe_s = din("pe_s", [2, 128, 256])
    sd_in = din("sd_in", [16, NH, 128, 128])
    sc_in = din("sc_in", [48, 3072])
    ck_in = din("ck_in", [16, 128, 256])
    cv_in = din("cv_in", [16, 128, 256])
    w_in = din("w_in", [D, 4112])
    w_out = din("w_out", [D, D])
    w_gu = din("w_gu", [2, D, 2 * FF])
    w_down = din("w_down", [2, FF, D])
    w_pp = din("w_pp", [2, 256, D])
    w_pg = din("w_pg", [2, D, D])
    w_kv = din("w_kv", [D, 512])
    w_q = din("w_q", [D, D])
    w_o = din("w_o", [D, D])
    packA = din("packA", [65, 128])
    packB = din("packB", [96, 128])
    vec8 = din("vec8", [1, 32])
    cst = din("cst", [128, NCC])
    smask_d = din("smask", [128, 2176])

    y_p = dout("y_p", [max(NTOK, 1), D])
    y_s = dout("y_s", [128, D])
    sd_p = dout("sd_p", [max(NSEQ, 1), NH, 128, 128])
    sd_s = dout("sd_s", [16, NH, 128, 128])
    sc_p = dout("sc_p", [max(NSEQ, 1) * 3, 3072])
    sc_s = dout("sc_s", [48, 3072])
    kw_p = dout("kw_p", [max(NSEQ, 1), 128, 256])
    kw_s = dout("kw_s", [16, 128, 256])
    vw_p = dout("vw_p", [max(NSEQ, 1), 128, 256])
    vw_s = dout("vw_s", [16, 128, 256])

    es = contextlib.ExitStack()
    with es:
        P = Prog(nc, es)

        def sb(name, shape, dt=F32, keys=None):
            t = es.enter_context(nc.sbuf_tensor(name, list(shape), dt))
            return V(t[:], keys if keys is not None else (name,))

        def ring(name, n, shape, dt=F32):
            return Ring([sb(f"{name}{i}", shape, dt) for i in range(n)])

        def psring(name, n, width=512):
            views = []
            for i in range(n):
                t = es.enter_context(nc.psum_tensor(f"{name}{i}", [128, 512], F32))
                views.append(V(t[:, 0:width], (f"{name}{i}",)))
                P.psum_keys.add(f"{name}{i}")
            return Ring(views)

        C = sb("C", [128, NCC])

        def cv(name):
            o, n = coffs[name]
            return C[:, o:o + n]
        identF = cv("ident")
        identB = sb("identB", [128, 128], BF16)
        ones1024 = sb("ones1024", [128, 128], BF16)
        ones128 = sb("ones128", [128, 128], BF16)
        ones1 = sb("ones1", [128, 128], BF16)
        kc = sb("kc", [128, 8])
        gains = sb("gains", [128, 65])
        convw = sb("convw", [128, 96])
        v8 = sb("v8", [128, 32])
        negA = sb("negA", [128, 8])
        h = VC(sb("h", [128, 8, 512]).ap, "h", 8)
        xn = VC(sb("xn", [128, 8, 512], BF16).ap, "xn", 8)
        sqr = ring("sqc", 2, [128, 512], BF16)
        rs = sb("rs", [128, 512])
        peT = [sb(f"peT{l}", [128, 2, 512], BF16) for l in range(2)]
        og = VC(sb("og", [128, 8, 512], BF16).ap, "og", 8)
        bigt = es.enter_context(nc.sbuf_tensor("big", [128, 6144], F32))
        big32 = V(bigt[:], ("big",))
        big16 = V(bigt[:].bitcast(BF16), ("big",))
        actb = big16[:, 0:NFC * 512].re("p (c n) -> p c n", c=NFC)
        S32 = VC(sb("S32", [128, 8, 128]).ap, "S32", 8)
        S16 = VC(sb("S16", [128, 8, 128], BF16).ap, "S16", 8)
        cc = sb("cc", [128, 24, 3])
        wab = sb("wab", [128, 8, 16], BF16)
        wslots = ring("ws", 3, [128, 4096], BF16)
        xst = ring("xst", 2, [128, 1024])
        pst = ring("pst", 2, [128, 256])
        pb = psring("pb", 8)
        mmr = pb

        class _Sm:
            def next(self):
                return pb.next()[:, 0:128]
        smr = _Sm()
        msr = smr

        GI = {"a_norm": 0, "ffn0": 1, "ffn1": 2, "ple0": 3, "ple1": 4, "kv": 5, "b": 6, "fin": 7}

        P.dma("sp", [(C.ap, cst)], "C", writes=C.keys)
        P.memset("dve", kc[:, 0:1], 0.0)
        P.memset("dve", kc[:, 1:2], -1.0)
        P.memset("dve", kc[:, 2:3], 1.0)
        P.memset("dve", ones1024, 1.0 / 1024.0)
        P.memset("dve", ones128, 1.0 / 128.0)
        P.memset("dve", ones1, 1.0)
        P.copy("dve", identB, identF)
        xs0 = xst.next()
        P.dma("sp", [(xs0.ap[0:65, 0:128], packA)], "xst0", writes=xs0.keys)
        ps = smr.next()
        P.mm(ps[:, 0:65], lhsT=xs0[0:65, 0:128], rhs=identF[0:65, 0:65])
        P.copy("dve", gains, ps[:, 0:65])
        xs1 = xst.next()
        P.dma("sp", [(xs1.ap[0:96, 0:128], packB)], "xst1", writes=xs1.keys)
        ps = smr.next()
        P.mm(ps[:, 0:96], lhsT=xs1[0:96, 0:128], rhs=identF[0:96, 0:96])
        P.copy("dve", convw, ps[:, 0:96])
        P.dma("sp", [(v8.ap, vec8.partition_broadcast(128))], "v8", writes=v8.keys)
        sinkp = sb("sinkp", [128, 16])
        negsink = sb("negsink", [128, 16])
        for k_ in range(4):
            for b_ in range(2):
                P.copy("dve", sinkp[:, 4 * k_ + 2 * b_:4 * k_ + 2 * b_ + 2],
                       v8[:, 16 + 4 * k_:20 + 4 * k_].re("p (a b) -> p a b", b=2)[:, :, b_])
        P.ts("dve", negsink, sinkp, -1.0, None, ALU.mult)
        P.act(negA, v8[:, 0:8], AF.Exp)
        P.ts("dve", negA, negA, -1.0, None, ALU.mult)
        P.dma("pool", [(wab.ap, w_in[:, 4096:4112].rearrange("(c p) n -> p c n", p=128))], "wab", writes=wab.keys)

        WT = {}

        def wdef(name, parts):
            WT[name] = (len(WT), parts, max(p[2] for p in parts))

        def part_dst(ap, lo, hi, kcn):
            return ap[:, lo:hi].rearrange("p (c n) -> p c n", c=kcn)

        def wload(name):
            ti, _, nu = WT[name]
            slot = wslots.next()
            P.dma(cfg.get("wq", "sp"), [(slot.ap[:, 0:nu], wsc[ti][:, 0:nu])], slot.keys[0], reads=(f"wsc{ti}",), writes=slot.keys)
            return slot

        def wsrc(w2d, c0, ncols):
            return w2d[:, c0:c0 + ncols].rearrange("(c p) n -> p c n", p=128)

        def rms(N):
            ps = mmr.next()
            for c in range(8):
                sqc = sqr.next()
                P.act(sqc[:, :N], h[:, c, :N], AF.Square)
                P.mm(ps[:, :N], lhsT=ones1024, rhs=sqc[:, :N], start=(c == 0), stop=(c == 7))
            P.act(rs[:, :N], ps[:, :N], AF.Ln, bias=EPS)
            P.act(rs[:, :N], rs[:, :N], AF.Exp, scale=-0.5)

        def normed(dst, gi, N):
            for c in range(8):
                P.stt("dve", dst[:, c, :N], h[:, c, :N], gains[:, gi * 8 + c:gi * 8 + c + 1], rs[:, :N],
                      ALU.mult, ALU.mult)

        R = {}
        t512 = ring("t512", 3, [128, 512])

        def wt(role, n=2, shape=(128, 128), dt=F32):
            if role not in R:
                R[role] = ring("r_" + role, n, list(shape), dt)
            return R[role].next()

        def ffn(l, N):
            rms(N)
            normed(xn, GI[f"ffn{l}"], N)
            for jj in range(NFC // 2):
                slot = wload(f"gu{l}_{jj}")
                for jo in range(2):
                    j = jj * 2 + jo
                    psg = mmr.next()
                    for c in range(8):
                        P.mm(psg[:, :N], lhsT=slot[:, c * 256 + jo * 128:c * 256 + jo * 128 + 128], rhs=xn[:, c, :N],
                             start=(c == 0), stop=(c == 7))
                    psu = mmr.next()
                    for c in range(8):
                        P.mm(psu[:, :N], lhsT=slot[:, 2048 + c * 256 + jo * 128:2048 + c * 256 + jo * 128 + 128],
                             rhs=xn[:, c, :N], start=(c == 0), stop=(c == 7))
                    sg = t512.next()
                    P.act(sg[:, :N], psg[:, :N], AF.Silu)
                    P.tt("dve", actb[:, j, :N], sg[:, :N], psu[:, :N], ALU.mult)
            for m in range(8):
                slot = wload(f"dn{l}_{m}")
                ps = mmr.next()
                for c in range(NFC):
                    P.mm(ps[:, :N], lhsT=slot[:, c * 128:(c + 1) * 128], rhs=actb[:, c, :N],
                         start=(c == 0), stop=(c == NFC - 1))
                P.tt("dve", h[:, m, :N], h[:, m, :N], ps[:, :N], ALU.add)

        def ple(l, N):
            rms(N)
            normed(xn, GI[f"ple{l}"], N)
            pslot = wload(f"pp{l}")
            for half in range(2):
                slot = wload(f"pg{l}_{half}")
                for mi in range(4):
                    m = half * 4 + mi
                    psg = mmr.next()
                    for c in range(8):
                        P.mm(psg[:, :N], lhsT=slot[:, c * 512 + mi * 128:c * 512 + mi * 128 + 128], rhs=xn[:, c, :N],
                             start=(c == 0), stop=(c == 7))
                    psp = mmr.next()
                    for c in range(2):
                        P.mm(psp[:, :N], lhsT=pslot[:, c * 1024 + m * 128:c * 1024 + m * 128 + 128],
                             rhs=peT[l][:, c, :N], start=(c == 0), stop=(c == 1))
                    sg = t512.next()
                    P.act(sg[:, :N], psg[:, :N], AF.Sigmoid)
                    P.tt("dve", sg[:, :N], sg[:, :N], psp[:, :N], ALU.mult)
                    P.tt("dve", h[:, m, :N], h[:, m, :N], sg[:, :N], ALU.add)

        def bank(w=512):
            return pb.next()[:, 0:w]

        def bmm(ps, lhs, rhs, T):
            for t in range(T):
                P.mm(ps[:, t * 128:(t + 1) * 128], lhsT=lhs(t), rhs=rhs(t))

        def inverse_upper(W, T, sample):
            nsq = 2 if sample else 3
            Lm, Um = W["Lm"], W["Um"]
            TW = T * 128

            def m3(v):
                return v[:, 0:TW].re("p (t n) -> p t n", t=T)

            def lv(M, lev):
                return lambda t: M[:, t, lev, :]

            def tl(v):
                return lambda t: v[:, t * 128:(t + 1) * 128]
            identT = identF.un(1).bc([128, T, 128])
            Ls = [lv(Lm, 0)]
            Us = [lv(Um, 0)]
            for k in range(nsq):
                needU = (k < nsq - 1) or not sample
                psL = bank(TW)
                bmm(psL, Us[k], Ls[k], T)
                if needU:
                    psU = bank(TW)
                    bmm(psU, Ls[k], Us[k], T)
                Ln = W[f"L{k + 1}"]
                P.copy("act", Ln[:, 0:TW], psL)
                Ls.append(tl(Ln))
                if needU:
                    Un = W[f"U{k + 1}"]
                    P.copy("dve", Un[:, 0:TW], psU)
                    Us.append(tl(Un))
                yield
            pu = [W["PUa"], W["PUb"]]
            pl = [W["PLa"], W["PLb"]] if not sample else None
            pi = 0
            PU = pu[0]
            P.tt(cfg.get("ce", "pool"), m3(PU), identT, Um[:, :, 0, :], ALU.subtract)
            if not sample:
                PL = pl[0]
                P.tt(cfg.get("ce", "pool"), m3(PL), identT, Lm[:, :, 0, :], ALU.subtract)
            for k in range(1, nsq + 1):
                psU = bank(TW)
                bmm(psU, Ls[k], tl(PU), T)
                if not sample:
                    psL = bank(TW)
                    bmm(psL, Us[k], tl(PL), T)
                PUn = W["TU"] if (sample and k == nsq) else pu[1 - pi]
                P.tt("dve", PUn[:, 0:TW], PU[:, 0:TW], psU, ALU.add)
                if not sample:
                    PLn = pl[1 - pi]
                    P.tt("dve", PLn[:, 0:TW], PL[:, 0:TW], psL, ALU.add)
                    PL = PLn
                pi = 1 - pi
                PU = PUn
                yield
            if sample:
                return
            Z, Z2 = W["Z"], W["Z2"]
            for lev in (1, 2, 3):
                last = lev == 3
                psa = bank(TW)
                bmm(psa, lv(Lm, lev), tl(PU), T)
                if not last:
                    psb = bank(TW)
                    bmm(psb, lv(Um, lev), tl(PL), T)
                P.copy("act", Z[:, 0:TW], psa)
                if not last:
                    P.copy("dve", Z2[:, 0:TW], psb)
                yield
                psa = bank(TW)
                bmm(psa, tl(PL), tl(Z), T)
                if not last:
                    psb = bank(TW)
                    bmm(psb, tl(PU), tl(Z2), T)
                PUn = W["TU"] if last else pu[1 - pi]
                P.tt("dve", PUn[:, 0:TW], PU[:, 0:TW], psa, ALU.subtract)
                if not last:
                    PLn = pl[1 - pi]
                    P.tt("dve", PLn[:, 0:TW], PL[:, 0:TW], psb, ALU.subtract)
                    PL = PLn
                pi = 1 - pi
                PU = PUn
                yield

        def layer_a(kind, seq, N, nt, first, last_g):
            sample = kind == "s"
            sfx = "_s" if sample else "_p"
            tri, seg = cv("tri" + sfx), cv("seg" + sfx)
            NL = 1 if sample else 4
            mLc = cv("mL" + sfx).re("p (l n) -> p l n", l=NL)
            mUc = cv("mU" + sfx).re("p (l n) -> p l n", l=NL)
            incU = cv("incU" + sfx)
            rms(N)
            normed(xn, GI["a_norm"], N)
            gtok = wt("gtok", 1, (128, 4, 8))
            btok = wt("btok", 1, (128, 4, 8))
            gct = wt("gct", 1, (128, 4, 8))
            dlt = wt("dlt", 1, (128, 4, 8))
            glt = wt("glt", 1, (128, 4, 8))
            ps = bank()
            for t in range(nt):
                for c in range(8):
                    P.mm(ps[:, t * 16:(t + 1) * 16], lhsT=xn[:, c, t * 128:(t + 1) * 128], rhs=wab[:, c, :],
                         start=(c == 0), stop=(c == 7))
            ps3 = ps[:, 0:nt * 16].re("p (t n) -> p t n", t=nt)
            tb = wt("tb4", 1, (128, 4, 8))[:, 0:nt, :]
            P.act(tb, ps3[:, :, 8:16], AF.Exp, scale=-1.0)
            P.ts("dve", tb, tb, 1.0, None, ALU.add)
            P.recip(btok[:, 0:nt, :], tb)
            xa = wt("xa4", 1, (128, 4, 8))[:, 0:nt, :]
            P.tt("dve", xa, ps3[:, :, 0:8], v8[:, 8:16].un(1).bc([128, nt, 8]), ALU.add)
            mxa = wt("mxa4", 1, (128, 4, 8))[:, 0:nt, :]
            P.ts("dve", mxa, xa, 0.0, None, ALU.max)
            ax = wt("ax4", 1, (128, 4, 8))[:, 0:nt, :]
            P.stt("dve", ax, mxa, -2.0, xa, ALU.mult, ALU.add)
            P.act(ax, ax, AF.Exp)
            P.act(ax, ax, AF.Ln, bias=1.0)
            P.tt("dve", xa, mxa, ax, ALU.add)
            P.tt("dve", gtok[:, 0:nt, :], xa, negA.un(1).bc([128, nt, 8]), ALU.mult)
            ps2 = bank()
            for t in range(nt):
                P.mm(ps2[:, t * 16:t * 16 + 8], lhsT=tri, rhs=gtok[:, t, :])
                P.mm(ps2[:, t * 16 + 8:t * 16 + 16], lhsT=seg, rhs=gtok[:, t, :])
            p23 = ps2[:, 0:nt * 16].re("p (t n) -> p t n", t=nt)
            P.copy("dve", gct[:, 0:nt, :], p23[:, :, 0:8])
            P.tt("dve", dlt[:, 0:nt, :], p23[:, :, 8:16], gct[:, 0:nt, :], ALU.subtract)
            P.act(dlt[:, 0:nt, :], dlt[:, 0:nt, :], AF.Exp)
            P.act(glt[:, 0:nt, :], p23[:, :, 8:16], AF.Exp)
            if sample:
                hist = smb[:, 0:1152].re("p (c q w) -> p c q w", c=24, q=16)
                ccs = Pm32[:, 0:1152].re("p (c q w) -> p c q w", c=24, q=16)
                for piece in range(3):
                    xs = xst.next()
                    P.dma("sp", [(xs.ap[0:48, :], sc_in[:, piece * 1024:(piece + 1) * 1024])], xs.keys[0],
                          writes=xs.keys)
                    for c8 in range(8):
                        ch = piece * 8 + c8
                        ps = smr.next()
                        P.mm(ps[:, 0:48], lhsT=xs[0:48, c8 * 128:(c8 + 1) * 128], rhs=identF[0:48, 0:48])
                        P.copy("act", hist[:, ch, :, :], ps[:, 0:48].re("p (q w) -> p q w", w=3))
            T = nt
            TW = T * 128
            if "gdnW" not in R:
                Wt = {}
                IDT = BF16 if cfg.get("inv16", True) else F32

                def cast(ap):
                    if IDT == F32:
                        return ap
                    n = ap.shape[-1]
                    return ap.bitcast(BF16)[:, 0:n]
                qa32 = qall.ap.rearrange("p c n -> p (c n)").bitcast(F32)
                for i, nm in enumerate(["D", "dL", "E", "Z"]):
                    a_ = qa32[:, i * 512:(i + 1) * 512]
                    Wt[nm] = V(cast(a_) if nm == "Z" else a_, ("g_" + nm,))
                assert IDT == BF16
                Wt["Z2"] = V(qa32[:, 3 * 512:4 * 512].bitcast(BF16)[:, 512:1024], ("g_Z2",))
                pt32 = PT.ap.rearrange("p b n -> p (b n)").bitcast(F32)
                for i, nm in enumerate(["L1", "L2"]):
                    Wt[nm] = V(cast(pt32[:, i * 512:(i + 1) * 512]), ("g_" + nm,))
                sU = sb("g_sU", [128, 4, 128])
                for i, nm in enumerate(["U1", "U2", "PUa", "PUb"]):
                    Wt[nm + "_p"] = V(cast(smb.ap[:, i * 512:(i + 1) * 512]), ("g_" + nm,))
                    Wt[nm + "_s"] = V(cast(sU.ap[:, i, :]), ("g_" + nm + "s",))
                Wt["osb"] = [sb(f"g_osb{i}", [128, 512], BF16) for i in range(3)]
                Wt["TU"] = [sb(f"g_TU{i}", [128, 512], IDT) for i in range(3)]
                Wt["vb"] = [sb(f"g_vb{i}", [128, 512], IDT) for i in range(3)]
                for nm in ["qkT", "qdec", "kbd", "kdec"]:
                    Wt[nm] = [sb(f"g_{nm}{i}", [128, 512], BF16) for i in range(3)]
                Wt["LmS"] = sb("g_LmS", [128, 1, 1, 128], IDT)
                Wt["UmS"] = sb("g_UmS", [128, 1, 1, 128], IDT)
                Wt["LmP"] = V(cast(big32.ap[:, 0:2048]).rearrange("p (t l n) -> p t l n", t=4, l=4), ("g_LmP",))
                Wt["UmP"] = V(cast(big32.ap[:, 2048:4096]).rearrange("p (t l n) -> p t l n", t=4, l=4), ("g_UmP",))
                for i, nm in enumerate(["L3", "U3", "PLa", "PLb"]):
                    Wt[nm] = V(cast(big32.ap[:, 4096 + i * 512:4096 + (i + 1) * 512]), ("g_" + nm,))
                R["gdnW"] = Wt
            Wt = R["gdnW"]

            def t3(v):
                return v[:, 0:TW].re("p (t n) -> p t n", t=T)

            def setup(hd):
                par = hd % 3
                W = dict(Wt)
                for nm in ["TU", "vb", "qkT", "qdec", "kbd", "kdec", "osb"]:
                    W[nm] = Wt[nm][par]
                W["Lm"] = Wt["LmS"] if sample else Wt["LmP"]
                W["Um"] = Wt["UmS"] if sample else Wt["UmP"]
                for nm in ["U1", "U2", "PUa", "PUb"]:
                    W[nm] = Wt[nm + ("_s" if sample else "_p")]
                slot = wload(f"in{hd}")
                pre1 = wt("pre", 1, (128, 515))
                qkv = []
                for s in range(3):
                    ch = s * 8 + hd
                    if sample:
                        pre = {s: pre1[:, 0:176].re("p (q t) -> p q t", q=16)}
                    else:
                        pre = {s: pre1}
                    ps = bank()
                    for c in range(8):
                        P.mm(ps[:, :N], lhsT=slot[:, s * 1024 + c * 128:s * 1024 + (c + 1) * 128], rhs=xn[:, c, :N],
                             start=(c == 0), stop=(c == 7))
                    cvo = t512.next()
                    if sample:
                        P.copy("act", pre[s][:, :, 0:3], hist[:, ch, :, :])
                        P.copy("act", pre[s][:, :, 3:11], ps[:, :128].re("p (q t) -> p q t", t=8))
                        cvv = cvo[:, :128].re("p (q t) -> p q t", t=8)
                        P.ts("dve", cvv, pre[s][:, :, 0:8], convw[:, ch:ch + 1], None, ALU.mult)
                        for w in range(1, 4):
                            P.stt("dve", cvv, pre[s][:, :, w:w + 8], convw[:, w * 24 + ch:w * 24 + ch + 1], cvv,
                                  ALU.mult, ALU.add)
                        P.copy("act", ccs[:, ch, :, :], pre[s][:, :, 8:11])
                    else:
                        if first:
                            P.memset("dve", pre[s][:, 0:3], 0.0)
                        else:
                            P.copy("act", pre[s][:, 0:3], cc[:, ch, :])
                        P.copy("act", pre[s][:, 3:3 + N], ps[:, :N])
                        P.ts("dve", cvo[:, :N], pre[s][:, 0:N], convw[:, ch:ch + 1], None, ALU.mult)
                        for w in range(1, 4):
                            P.stt("dve", cvo[:, :N], pre[s][:, w:w + N], convw[:, w * 24 + ch:w * 24 + ch + 1],
                                  cvo[:, :N], ALU.mult, ALU.add)
                        P.copy("act", cc[:, ch, :], pre[s][:, N:N + 3])
                    P.act(cvo[:, :N], cvo[:, :N], AF.Silu)
                    if s < 2:
                        sqq = wt("sqq", 1, (128, 512), BF16)
                        P.act(sqq[:, :N], cvo[:, :N], AF.Square)
                        pss = bank()
                        P.mm(pss[:, :N], lhsT=ones1, rhs=sqq[:, :N])
                        rn = t512.next()
                        P.act(rn[:, :N], pss[:, :N], AF.Ln, bias=1e-6)
                        P.act(rn[:, :N], rn[:, :N], AF.Exp, scale=-0.5)
                        o16 = wt("qn" if s == 0 else "kn", 2, (128, 512), BF16)
                        P.stt("dve", o16[:, :N], cvo[:, :N], (128.0 ** -0.5) if s == 0 else 1.0, rn[:, :N],
                              ALU.mult, ALU.mult)
                    else:
                        o16 = wt("vn", 2, (128, 512), BF16)
                        P.copy("dve", o16[:, :N], cvo[:, :N])
                    qkv.append(o16)
                    yield
                qn, kn, vn = qkv
                ps = bank()
                for c in range(8):
                    P.mm(ps[:, :N], lhsT=slot[:, 3072 + c * 128:3072 + (c + 1) * 128], rhs=xn[:, c, :N],
                         start=(c == 0), stop=(c == 7))
                zs = wt("zs", 3, (128, 512), BF16)
                P.act(zs[:, :N], ps[:, :N], AF.Silu)
                W["zs"] = zs
                yield
                return (hd, W, qn, kn, vn)

            def halves(hd, W, qn, kn, vn):
                NHF = 2 if (T == 4 and not cfg.get("nhf1")) else 1
                Th = T // NHF
                gens = []
                Wfull = dict(W)
                for hf in range(NHF):
                    t0 = hf * Th
                    Wh = {}
                    USED = ["D", "dL", "E", "Z", "Z2", "L1", "L2", "L3", "U1", "U2", "U3", "PUa", "PUb", "PLa", "PLb",
                            "TU", "vb", "qkT", "qdec", "kbd", "kdec", "Lm", "Um"]
                    for nm in USED:
                        v = W[nm]
                        ks = tuple(k + f"_{hf}" for k in v.keys)
                        if nm in ("Lm", "Um"):
                            Wh[nm] = V(v.ap[:, t0:t0 + Th], ks)
                        elif nm in ("zs", "gls"):
                            Wh[nm] = v
                        else:
                            Wh[nm] = V(v.ap[:, t0 * 128:(t0 + Th) * 128], ks)
                        if nm not in ("zs", "gls"):
                            prevk = Wfull[nm].keys if hf > 0 else ()
                            Wfull[nm] = V(v.ap, tuple(prevk) + ks)
                    gens.append(half(hd, Wh, t0, Th, qn, kn, vn, Wfull))
                active = list(gens)
                while active:
                    for g_ in list(active):
                        try:
                            next(g_)
                        except StopIteration:
                            active.remove(g_)
                    yield
                return Wfull

            def half(hd, W, t0, T, qn, kn, vn, Wfull):
                TW = T * 128
                cs = slice(t0 * 128, t0 * 128 + TW)

                def t3(v):
                    return v[:, 0:TW].re("p (t n) -> p t n", t=T)

                def kt(t):
                    return kn[:, (t0 + t) * 128:(t0 + t + 1) * 128]

                def qt(t):
                    return qn[:, (t0 + t) * 128:(t0 + t + 1) * 128]
                ps_kk = bank(TW)
                bmm(ps_kk, kt, kt, T)
                ps_qk = bank(TW)
                bmm(ps_qk, kt, qt, T)
                ps_G = bank(TW)
                bmm(ps_G, lambda t: gtok[:, t0 + t, hd:hd + 1].bc([128, 128]), lambda t: tri, T)
                ps_B = bank(TW)
                bmm(ps_B, lambda t: btok[:, t0 + t, hd:hd + 1].bc([128, 128]), lambda t: identF, T)
                gcb = gct[:, t0:t0 + T, hd:hd + 1].bc([128, T, 128])
                bcb = btok[:, t0:t0 + T, hd:hd + 1].bc([128, T, 128])
                Dm, dL, E, Z = W["D"], W["dL"], W["E"], W["Z"]
                P.tt("dve", t3(Dm), t3(ps_G), gcb, ALU.subtract)
                P.ts("dve", dL[:, 0:TW], Dm[:, 0:TW], 0.0, None, ALU.max)
                P.act(dL[:, 0:TW], dL[:, 0:TW], AF.Exp, scale=-1.0)
                P.ts("dve", Dm[:, 0:TW], Dm[:, 0:TW], 0.0, None, ALU.min)
                P.act(Dm[:, 0:TW], Dm[:, 0:TW], AF.Exp)
                P.act(E[:, 0:TW], ps_G, AF.Exp)
                if sample:
                    gls = wt("gls", 3, (128, 16))
                    P.act(gls, ps_G[:, 0:128].re("p (s t) -> p s t", t=8)[:, :, 7], AF.Exp)
                    Wfull["gls"] = gls
                P.tt("dve", dL[:, 0:TW], dL[:, 0:TW], ps_kk, ALU.mult)
                P.tt("dve", t3(dL), t3(dL), bcb, ALU.mult)
                P.tt(cfg.get("ce", "pool"), t3(Z), t3(Dm), incU.un(1).bc([128, T, 128]), ALU.mult)
                P.tt("dve", W["qkT"][:, 0:TW], Z[:, 0:TW], ps_qk, ALU.mult)
                P.tt("dve", Dm[:, 0:TW], Dm[:, 0:TW], ps_kk, ALU.mult)
                P.tt("dve", Dm[:, 0:TW], Dm[:, 0:TW], ps_B, ALU.mult)
                P.tt(cfg.get("ce", "pool"), W["qdec"][:, 0:TW], qn[:, cs], E[:, 0:TW], ALU.mult)
                P.tt("dve", E[:, 0:TW], E[:, 0:TW], ps_B, ALU.mult)
                P.tt(cfg.get("ce", "pool"), W["kbd"][:, 0:TW], kn[:, cs], E[:, 0:TW], ALU.mult)
                yield
                Lm, Um = W["Lm"], W["Um"]
                P.tt(cfg.get("ce", "pool"), Lm, t3(dL).un(2).bc([128, T, NL, 128]), mLc.un(1).bc([128, T, NL, 128]), ALU.mult)
                P.tt(cfg.get("ce", "pool"), Um, t3(Dm).un(2).bc([128, T, NL, 128]), mUc.un(1).bc([128, T, NL, 128]), ALU.mult)
                ps = bank(TW)
                bmm(ps, kt, lambda t: identB, T)
                P.tt("dve", t3(W["kdec"]), t3(ps), dlt[:, t0:t0 + T, hd:hd + 1].bc([128, T, 128]), ALU.mult)
                ps = bank(TW)
                bmm(ps, lambda t: vn[:, (t0 + t) * 128:(t0 + t + 1) * 128], lambda t: identB, T)
                P.tt("dve", t3(W["vb"]), t3(ps), bcb, ALU.mult)
                yield
                yield from inverse_upper(W, T, sample)

            def chain(hd, W):
                tl = lambda v, t: v[:, t * 128:(t + 1) * 128]
                osb = W["osb"]
                if sample:
                    s0f = V(big32.ap[:, 0:2048].rearrange("p (s v) -> p s v", s=16), ("bg_s0f",))
                    s0b = V(big16.ap[:, 8192:10240].rearrange("p (s v) -> p s v", s=16), ("bg_s0b",))
                    src = sd_in[:, hd].rearrange("s d v -> d s v")
                    P.dma("sp", [(s0f.ap, src)], "s0f", writes=s0f.keys)
                    P.copy("act", s0b, s0f)
                    yield
                for t in range(T):
                    kbd, qdec, qkT, kdec = tl(W["kbd"], t), tl(W["qdec"], t), tl(W["qkT"], t), tl(W["kdec"], t)
                    vb, TU = tl(W["vb"], t), tl(W["TU"], t)
                    r = wt("r16", 2, (128, 128), BF16) if cfg.get("inv16", True) else wt("r", 2)
                    if not sample:
                        ps = bank(128)
                        P.mm(ps, lhsT=kbd, rhs=S16[:, hd, :])
                        P.tt("dve", r, vb, ps, ALU.subtract)
                    else:
                        ps = bank(128)
                        for s_ in range(16):
                            P.mm(ps[:, 8 * s_:8 * s_ + 8], lhsT=s0b[:, s_, :], rhs=kbd[:, 8 * s_:8 * s_ + 8])
                        rT = wt("rT", 1)
                        P.copy("act", rT, ps)
                        ps = bank(128)
                        P.mm(ps, lhsT=rT, rhs=identF)
                        P.tt("dve", r, vb, ps, ALU.subtract)
                    yield
                    ps = bank(128)
                    P.mm(ps, lhsT=TU, rhs=r)
                    u16 = wt("u16", 2, (128, 128), BF16)
                    P.copy("act", u16, ps)
                    yield
                    if not sample:
                        ps_o = bank(128)
                        P.mm(ps_o, lhsT=S16[:, hd, :], rhs=qdec, start=True, stop=False)
                        P.mm(ps_o, lhsT=u16, rhs=qkT, start=False, stop=True)
                        P.copy("act", tl(osb, t), ps_o)
                        ps = bank(128)
                        P.mm(ps, lhsT=kdec, rhs=u16)
                        P.stt("dve", S32[:, hd, :], S32[:, hd, :], glt[:, t, hd:hd + 1], ps, ALU.mult, ALU.add)
                        P.copy("dve", S16[:, hd, :], S32[:, hd, :])
                    else:
                        gls = W["gls"]
                        ps_o = bank(128)
                        for s_ in range(16):
                            P.mm(ps_o[:, 8 * s_:8 * s_ + 8], lhsT=s0b[:, s_, :], rhs=qdec[:, 8 * s_:8 * s_ + 8])
                        ps_o2 = bank(128)
                        P.mm(ps_o2, lhsT=u16, rhs=qkT)
                        P.copy("act", tl(osb, t), ps_o)
                        P.tt("dve", tl(osb, t), tl(osb, t), ps_o2, ALU.add)
                        kdm = V(big16.ap[:, 10240:12288].rearrange("p (s v) -> p s v", s=16), ("bg_kdm",))
                        P.tt("dve", kdm, kdec.un(1).bc([128, 16, 128]),
                             cv("segmask").un(2).bc([128, 16, 128]), ALU.mult)
                        snew = V(big32.ap[:, 2048:4096].rearrange("p (s v) -> p s v", s=16), ("bg_snew",))
                        for b in range(4):
                            ps = bank()
                            for s4 in range(4):
                                s_ = 4 * b + s4
                                P.mm(ps[:, s4 * 128:(s4 + 1) * 128], lhsT=kdm[:, s_, :], rhs=u16)
                            P.tt("dve", snew[:, 4 * b:4 * b + 4, :], s0f[:, 4 * b:4 * b + 4, :],
                                 gls[:, 4 * b:4 * b + 4].un(2).bc([128, 4, 128]), ALU.mult)
                            P.tt("dve", snew[:, 4 * b:4 * b + 4, :], snew[:, 4 * b:4 * b + 4, :],
                                 ps.re("p (s v) -> p s v", s=4), ALU.add)
                        P.dma(cfg.get("oq", "act"), [(sd_s[:, hd].rearrange("s d v -> d s v"), snew.ap)], "snew", reads=snew.keys)
                    yield
                sq2 = wt("sq2c", 2, (128, 512), BF16)
                P.act(sq2[:, 0:TW], osb[:, 0:TW], AF.Square)
                ps = bank(TW)
                P.mm(ps, lhsT=ones128, rhs=sq2[:, 0:TW])
                ro = t512.next()
                P.act(ro[:, 0:TW], ps, AF.Ln, bias=EPS)
                P.act(ro[:, 0:TW], ro[:, 0:TW], AF.Exp, scale=-0.5)
                P.stt("dve", osb[:, 0:TW], osb[:, 0:TW], gains[:, 64:65], ro[:, 0:TW], ALU.mult, ALU.mult)
                P.tt("dve", og[:, hd, 0:TW], osb[:, 0:TW], W["zs"][:, 0:TW], ALU.mult)
                yield

            def drive(gen, box=None):
                try:
                    next(gen)
                    return True
                except StopIteration as e:
                    if box is not None:
                        box.append(e.value)
                    return False

            chains = {}

            def step_chains():
                for k_ in sorted(chains):
                    if not drive(chains[k_]):
                        del chains[k_]
            pending = None
            for hd in range(NH + 1):
                while hd < NH and ((hd - 3) in chains or len(chains) > 2):
                    step_chains()
                pg_ = setup(hd) if hd < NH else None
                pbox = []
                hg_ = halves(*pending) if pending is not None else None
                hbox = []
                while pg_ is not None or hg_ is not None:
                    step_chains()
                    if hg_ is not None and not drive(hg_, hbox):
                        hg_ = None
                    if pg_ is not None and not drive(pg_, pbox):
                        pg_ = None
                if pending is not None:
                    chains[pending[0]] = chain(pending[0], hbox[0])
                pending = pbox[0] if hd < NH else None
            while chains:
                step_chains()
            if not sample and last_g:
                P.dma(cfg.get("oq", "act"), [(sd_p[seq].rearrange("h d v -> d h v"), S32.ap)], "S32o", reads=S32.keys)
                stg = xst.next()
                for b in range(6):
                    ps = mmr.next()
                    for c4 in range(4):
                        ch = b * 4 + c4
                        P.mm(ps[0:3, c4 * 128:(c4 + 1) * 128], lhsT=cc[:, ch, :], rhs=identF)
                    P.copy("act", stg[0:3, (b % 2) * 512:(b % 2) * 512 + 512], ps[0:3, :])
                    if b % 2 == 1:
                        pc = b // 2
                        P.dma(cfg.get("oq", "act"), [(sc_p[seq * 3:seq * 3 + 3, pc * 1024:(pc + 1) * 1024], stg.ap[0:3, :])],
                              stg.keys[0] + "o", reads=stg.keys)
                        if b < 5:
                            stg = xst.next()
            if sample:
                stg = xst.next()
                for b in range(6):
                    ps = mmr.next()
                    for c4 in range(4):
                        ch = b * 4 + c4
                        P.mm(ps[0:48, c4 * 128:(c4 + 1) * 128], lhsT=ccs[:, ch, :, :].re("p q w -> p (q w)"),
                             rhs=identF)
                    P.copy("act", stg[0:48, (b % 2) * 512:(b % 2) * 512 + 512], ps[0:48, :])
                    if b % 2 == 1:
                        pc = b // 2
                        P.dma(cfg.get("oq", "act"), [(sc_s[:, pc * 1024:(pc + 1) * 1024], stg.ap[0:48, :])], stg.keys[0] + "o",
                              reads=stg.keys)
                        if b < 5:
                            stg = xst.next()
            for half in range(2):
                slot = wload(f"out{half}")
                for mi in range(4):
                    m = half * 4 + mi
                    ps = mmr.next()
                    for c in range(8):
                        P.mm(ps[:, :N], lhsT=slot[:, c * 512 + mi * 128:c * 512 + mi * 128 + 128], rhs=og[:, c, :N],
                             start=(c == 0), stop=(c == 7))
                    P.tt("dve", h[:, m, :N], h[:, m, :N], ps[:, :N], ALU.add)

        kall = sb("kall", [128, 4, 640], BF16)
        vtok = sb("vtok", [128, 5, 4, 2, 64], BF16)
        qall = sb("qall", [128, 8, 512], BF16)
        smb = sb("smb", [128, 2176])
        pmt = es.enter_context(nc.sbuf_tensor("Pm", [128, 1152], F32))
        Pm32 = V(pmt[:], ("Pm",))
        Pm = V(pmt[:].bitcast(BF16), ("Pm",))
        PT = sb("PT", [128, 17, 128], BF16)
        if SAMPLE:
            kck = V(big16.ap[:, 4352:6400], ("big",))
            ckb = V(big16.ap[:, 6400:8448].rearrange("p (s o d) -> p s o d", s=16, o=2), ("big",))
            vcb = V(big16.ap[:, 8448:10496].rearrange("p (s o d) -> p s o d", s=16, o=2), ("big",))

        def layer_b(kind, seq, N, nt, first, last_g):
            sample = kind == "s"

            def stop(k):
                if cfg.get("stop") == k:
                    raise StopBuild()
            rms(N)
            normed(xn, GI["kv"], N)
            slot = wload("kv")
            kslot = wload("kdup")
            if first and not sample:
                P.memset("dve", kall[:, :, 0:128], 0.0)
                P.memset("dve", vtok[:, 0], 0.0)
            for kvh in range(4):
                ps = mmr.next()
                for c in range(8):
                    P.mm(ps[:, :N], lhsT=kslot[:, c * 512 + kvh * 128:c * 512 + kvh * 128 + 128], rhs=xn[:, c, :N],
                         start=(c == 0), stop=(c == 7))
                P.copy("act", kall[:, kvh, 128:128 + N], ps[:, :N])
            stop(41)
            for t in range(nt):
                ps = mmr.next()
                for c in range(8):
                    P.mm(ps, lhsT=xn[:, c, t * 128:(t + 1) * 128], rhs=slot[:, c * 512:(c + 1) * 512],
                         start=(c == 0), stop=(c == 7))
                for o in range(2):
                    if cfg.get("skip_vtok"):
                        continue
                    P.copy("dve", vtok[:, t + 1, :, o, :], ps[:, 256:512].re("p (k d) -> p k d", k=4))
                if (sample or (last_g and t == nt - 1)) and not cfg.get("skip_kvt"):
                    kvt = t512.next()
                    P.copy("act", kvt, ps)
                    if sample:
                        pairs = []
                        for sq_ in range(16):
                            pairs.append((kw_s[sq_, 120:128, :], kvt.ap[sq_ * 8:(sq_ + 1) * 8, 0:256]))
                            pairs.append((vw_s[sq_, 120:128, :], kvt.ap[sq_ * 8:(sq_ + 1) * 8, 256:512]))
                        P.dma(cfg.get("oq", "act"), pairs, kvt.keys[0] + "o", reads=kvt.keys)
                    else:
                        if cfg.get("kvt_nodma"):
                            pass
                        elif cfg.get("kvt_one"):
                            P.dma("sp", [(kw_p[seq], kvt.ap[:, 0:256])], kvt.keys[0] + "o", reads=kvt.keys)
                        elif cfg.get("kvt_two"):
                            P.dma("sp", [(kw_p[seq], kvt.ap[:, 0:256])], kvt.keys[0] + "o", reads=kvt.keys)
                            P.dma("sp", [(vw_p[seq], kvt.ap[:, 256:512])], kvt.keys[0] + "o2", reads=kvt.keys)
                        else:
                            P.dma(cfg.get("oq", "act"), [(kw_p[seq], kvt.ap[:, 0:256]), (vw_p[seq], kvt.ap[:, 256:512])],
                                  kvt.keys[0] + "o", reads=kvt.keys)
            stop(42)
            normed(xn, GI["b"], N)
            for half in range(2):
                slot = wload(f"q{half}")
                for mi in range(4):
                    m = half * 4 + mi
                    ps = mmr.next()
                    for c in range(8):
                        P.mm(ps[:, :N], lhsT=slot[:, c * 512 + mi * 128:c * 512 + mi * 128 + 128], rhs=xn[:, c, :N],
                             start=(c == 0), stop=(c == 7))
                    P.copy("act", qall[:, m, :N], ps[:, :N], scale=0.125)
            stop(43)
            if sample:
                smask = big32[:, 0:2176]
                P.dma("sp", [(smask.ap, smask_d)], "smaskd", writes=smask.keys)
                P.dma("sp", [(kw_s[:, 0:120, :], ck_in[:, 8:128, :]), (vw_s[:, 0:120, :], cv_in[:, 8:128, :])], "cpass")
            def attn_unit(t, kvh, par):
                tc_ = slice(t * 128, (t + 1) * 128)
                hq0 = 4 * kvh
                smb_p = V(smb.ap[:, par * 1024:(par + 1) * 1024], (f"smb_{par}",))
                Pm_p = V(Pm.ap[:, par * 1024:(par + 1) * 1024], (f"Pm_{par}",))
                PT_p = V(PT.ap[:, par * 8:(par + 1) * 8, :], (f"PT_{par}",))
                mk = cv("amask0") if (first and t == 0) else cv("amask")
                vbl = [vtok[:, t, kvh], vtok[:, t + 1, kvh]]
                gmap = [0, 2, 1, 3]
                for b2 in range(2):
                    ps = bank()
                    for gi in range(2):
                        g = gmap[b2 * 2 + gi]
                        ph = slice((g % 2) * 64, (g % 2) * 64 + 64)
                        P.mm(ps[:, gi * 256:(gi + 1) * 256], lhsT=qall[ph, (hq0 + g) // 2, tc_],
                             rhs=kall[ph, kvh, t * 128:t * 128 + 256])
                    P.tt("dve", smb_p[:, b2 * 512:(b2 + 1) * 512].re("p (g k) -> p g k", g=2),
                         ps.re("p (g k) -> p g k", g=2), mk.un(1).bc([128, 2, 256]), ALU.add)
                stop(51)
                yield
                mx4 = wt("mx4", 2, (128, 4))
                P.red("dve", mx4, smb_p.re("p (g k) -> p g k", g=4), ALU.max)
                nmx4 = wt("nmx4", 2, (128, 4))
                P.stt("dve", nmx4, mx4, -1.0, negsink[:, hq0:hq0 + 4], ALU.mult, ALU.min)
                rs4 = wt("rs4", 2, (128, 4))
                for g in range(4):
                    P.act(Pm_p[:, g * 256:(g + 1) * 256], smb_p[:, g * 256:(g + 1) * 256], AF.Exp,
                          bias=nmx4[:, g:g + 1], accum=rs4[:, g:g + 1])
                stop(52)
                yield
                es4 = wt("es4", 2, (128, 4))
                P.tt("dve", es4, sinkp[:, hq0:hq0 + 4], nmx4, ALU.add)
                P.act(es4, es4, AF.Exp)
                P.tt("dve", es4, es4, rs4, ALU.add)
                rl4 = wt("rl4", 2, (128, 4))
                P.recip(rl4, es4)
                dg4 = wt("dg4", 2, (128, 4, 128), BF16)
                P.tt("dve", dg4, identB.un(1).bc([128, 4, 128]), rl4.un(2).bc([128, 4, 128]), ALU.mult)
                stop(53)
                yield
                for b2 in range(2):
                    ps = bank()
                    for gi in range(2):
                        g = b2 * 2 + gi
                        for kb in range(2):
                            j = gi * 2 + kb
                            P.mm(ps[:, j * 128:(j + 1) * 128],
                                 lhsT=Pm_p[:, g * 256 + kb * 128:g * 256 + (kb + 1) * 128], rhs=dg4[:, g, :])
                    P.copy("act" if b2 == 0 else "dve", PT_p[:, b2 * 4:(b2 + 1) * 4, :],
                           ps.re("p (b n) -> p b n", b=4))
                stop(54)
                yield
                pso = bank()
                for g in range(4):
                    for kb in range(2):
                        P.mm(pso[:, g * 128:(g + 1) * 128], lhsT=vbl[kb].re("p o d -> p (o d)"),
                             rhs=PT_p[:, g * 2 + kb, :], start=(kb == 0), stop=(kb == 1))
                for sl in range(4):
                    g = gmap[sl]
                    ph = slice((g % 2) * 64, (g % 2) * 64 + 64)
                    P.copy("act" if sl % 2 == 0 else "dve", og[ph, (hq0 + g) // 2, tc_], pso[ph, sl * 128:(sl + 1) * 128])
                yield

            if not sample:
                units = [attn_unit(t, kvh, (t * 4 + kvh) % 2) for t in range(nt) for kvh in range(4)]
                active = []
                while units or active:
                    while units and len(active) < cfg.get("aw", 2):
                        active.append([units.pop(0), 0])
                    for a_ in list(active):
                        try:
                            next(a_[0])
                            a_[1] += 1
                        except StopIteration:
                            active.remove(a_)
            for t in range(nt if sample else 0):
                tc_ = slice(t * 128, (t + 1) * 128)
                for hq in range(16):
                    kvh = hq // 4
                    hf = hq % 2
                    ph = slice(hf * 64, (hf + 1) * 64)
                    if sample and hq % 4 == 0:
                        pk, pv = [], []
                        for o in range(2):
                            pk.append((ckb.ap[:, :, o, :], ck_in[:, :, kvh * 64:(kvh + 1) * 64].rearrange("s n d -> n s d")))
                            pv.append((vcb.ap[:, :, o, :], cv_in[:, :, kvh * 64:(kvh + 1) * 64].rearrange("s n d -> n s d")))
                        P.dma("pool", pk, "ckb", writes=ckb.keys)
                        P.dma("pool", pv, "vcb", writes=vcb.keys)
                        for s4 in range(4):
                            ps = mmr.next()
                            for si in range(4):
                                s_ = s4 * 4 + si
                                P.mm(ps[:, si * 128:(si + 1) * 128], lhsT=ckb[:, s_].re("p o d -> p (o d)"), rhs=identB)
                            P.copy("act", kck[:, s4 * 512:(s4 + 1) * 512], ps)
                    if sample:
                        pieces = [(kck[ph, p * 512:(p + 1) * 512], smask[:, p * 512:(p + 1) * 512], 512)
                                  for p in range(4)]
                        pieces.append((kall[ph, kvh, 128:256], smask[:, 2048:2176], 128))
                        vbl = [vcb[:, s_] for s_ in range(16)] + [vtok[:, 1, kvh]]
                    else:
                        mk = cv("amask0") if (first and t == 0) else cv("amask")
                        pieces = [(kall[ph, kvh, t * 128:t * 128 + 256], mk, 256)]
                        vbl = [vtok[:, t, kvh], vtok[:, t + 1, kvh]]
                    off = 0
                    for (kk, mk, w) in pieces:
                        ps = mmr.next()
                        P.mm(ps[:, :w], lhsT=qall[ph, hq // 2, tc_], rhs=kk)
                        P.tt("dve", smb[:, off:off + w], ps[:, :w], mk, ALU.add)
                        off += w
                    nk = off
                    stop(44)
                    mx = wt("mx", 2, (128, 1))
                    P.red("dve", mx, smb[:, :nk], ALU.max)
                    nmx = wt("nmx", 2, (128, 1))
                    P.ts("dve", nmx, mx, v8[:, 16 + hq:17 + hq], kc[:, 1:2], ALU.max, ALU.mult)
                    rsum = wt("rsum", 2, (128, 1))
                    P.act(Pm[:, :nk], smb[:, :nk], AF.Exp, bias=nmx, accum=rsum)
                    esk = wt("esk", 2, (128, 1))
                    P.act(esk, v8[:, 16 + hq:17 + hq], AF.Exp, bias=nmx)
                    P.tt("dve", esk, esk, rsum, ALU.add)
                    rl = wt("rl", 2, (128, 1))
                    P.recip(rl, esk)
                    stop(45)
                    dg = wt("dg", 2, (128, 128), BF16)
                    P.ts("dve", dg, identB, rl, None, ALU.mult)
                    nkb = nk // 128
                    for b0 in range(0, nkb, 4):
                        nb = min(4, nkb - b0)
                        ps = mmr.next()
                        for bi in range(nb):
                            kb = b0 + bi
                            P.mm(ps[:, bi * 128:(bi + 1) * 128], lhsT=Pm[:, kb * 128:(kb + 1) * 128], rhs=dg)
                        P.copy("act", PT[:, b0:b0 + nb, :], ps[:, :nb * 128].re("p (b n) -> p b n", b=nb))
                    pso = smr.next()
                    for kb in range(nkb):
                        P.mm(pso, lhsT=vbl[kb].re("p o d -> p (o d)"), rhs=PT[:, kb, :], start=(kb == 0),
                             stop=(kb == nkb - 1))
                    P.copy("act", og[ph, hq // 2, tc_], pso[ph, :])
                    stop(46)
                    if hq == 1:
                        stop(47)
            if not sample:
                P.copy("act", kall[:, :, 0:128], kall[:, :, N:N + 128])
                P.copy("act", vtok[:, 0], vtok[:, nt])
            for half in range(2):
                slot = wload(f"o{half}")
                for mi in range(4):
                    m = half * 4 + mi
                    ps = mmr.next()
                    for c in range(8):
                        P.mm(ps[:, :N], lhsT=slot[:, c * 512 + mi * 128:c * 512 + mi * 128 + 128], rhs=og[:, c, :N],
                             start=(c == 0), stop=(c == 7))
                    P.tt("dve", h[:, m, :N], h[:, m, :N], ps[:, :N], ALU.add)

        def group(kind, seq, g, nt, first, last_g):
            N = nt * 128
            if kind == "p":
                r0 = (seq * TPS + g * 4) * 128
                xd, pd, yd = x_p, pe_p, y_p
            else:
                r0 = 0
                xd, pd, yd = x_s, pe_s, y_s
            for t in range(nt):
                rows = slice(r0 + t * 128, r0 + (t + 1) * 128)
                xs = xst.next()
                P.dma("sp", [(xs.ap, xd[rows, :])], xs.keys[0], writes=xs.keys)
                for half in range(2):
                    ps = mmr.next()
                    for c4 in range(4):
                        c = half * 4 + c4
                        P.mm(ps[:, c4 * 128:(c4 + 1) * 128], lhsT=xs[:, c * 128:(c + 1) * 128], rhs=identF)
                    P.copy("act", h[:, half * 4:(half + 1) * 4, t * 128:(t + 1) * 128],
                           ps.re("p (c n) -> p c n", c=4))
                for l in range(2):
                    pt = pst.next()
                    P.dma("sp", [(pt.ap, pd[l, rows, :])], pt.keys[0], writes=pt.keys)
                    ps = mmr.next()
                    for c in range(2):
                        P.mm(ps[:, c * 128:(c + 1) * 128], lhsT=pt[:, c * 128:(c + 1) * 128], rhs=identF)
                    P.copy("act", peT[l][:, :, t * 128:(t + 1) * 128], ps[:, 0:256].re("p (c n) -> p c n", c=2))
            def stop(k):
                if cfg.get("stop") == k:
                    raise StopBuild()
            stop(1)
            if kind == "p" and first:
                P.memset("dve", S32, 0.0)
                P.memset("dve", S16, 0.0)
            BGK = ["big", "bg_s0f", "bg_s0b", "bg_kdm", "bg_snew"] + [f"g_{n_}_{i_}" for i_ in range(2) for n_ in
                                                                       ["LmP", "UmP", "L3", "U3", "PLa", "PLb"]]
            if kind == "s":
                P.op("dve", lambda e: e.memset(kc.ap[:, 6:7], 0.0), reads=(), writes=tuple(BGK + ["kcy"]))
            layer_a(kind, seq, N, nt, first, last_g)
            if kind == "s":
                P.op("dve", lambda e: e.memset(kc.ap[:, 6:7], 0.0), reads=(), writes=tuple(BGK + ["kcy"]))
            stop(2)
            ffn(0, N)
            stop(3)
            ple(0, N)
            stop(4)
            layer_b(kind, seq, N, nt, first, last_g)
            stop(5)
            ffn(1, N)
            ple(1, N)
            stop(6)
            rms(N)
            for c in range(8):
                P.stt("dve", h[:, c, :N], h[:, c, :N], gains[:, GI["fin"] * 8 + c:GI["fin"] * 8 + c + 1], rs[:, :N],
                      ALU.mult, ALU.mult)
            for t in range(nt):
                rows = slice(r0 + t * 128, r0 + (t + 1) * 128)
                ys = xst.next()
                for half in range(2):
                    ps = mmr.next()
                    for c4 in range(4):
                        c = half * 4 + c4
                        P.mm(ps[:, c4 * 128:(c4 + 1) * 128], lhsT=h[:, c, t * 128:(t + 1) * 128], rhs=identF)
                    P.copy("act", ys[:, half * 512:(half + 1) * 512], ps)
                P.dma(cfg.get("oq", "act"), [(yd[rows, :], ys.ap)], ys.keys[0] + "o", reads=ys.keys)

        def simple(w2d, c0, ncols, kcn):
            return [(lambda ap, n=ncols, k=kcn: part_dst(ap, 0, k * n, k), wsrc(w2d, c0, ncols), kcn * ncols)]
        for hd in range(NH):
            wdef(f"in{hd}", [(lambda ap, s_=s_: part_dst(ap, s_ * 1024, (s_ + 1) * 1024, 8),
                              wsrc(w_in, s_ * 1024 + hd * 128, 128), (s_ + 1) * 1024) for s_ in range(4)])
        for half in range(2):
            wdef(f"out{half}", simple(w_out, half * 512, 512, 8))
        for l in range(2):
            for jj in range(NFC // 2):
                wdef(f"gu{l}_{jj}", [(lambda ap: part_dst(ap, 0, 2048, 8), wsrc(w_gu[l], jj * 256, 256), 2048),
                                     (lambda ap: part_dst(ap, 2048, 4096, 8), wsrc(w_gu[l], FF + jj * 256, 256), 4096)])
            for m in range(8):
                wdef(f"dn{l}_{m}", simple(w_down[l], m * 128, 128, NFC))
            wdef(f"pp{l}", simple(w_pp[l], 0, 1024, 2))
            for half in range(2):
                wdef(f"pg{l}_{half}", simple(w_pg[l], half * 512, 512, 8))
        wdef("kv", simple(w_kv, 0, 512, 8))
        wdef("kdup", [(lambda ap, kvh=kvh, o=o: ap.rearrange("p (c k o d) -> p c k o d", c=8, k=4, o=2)[:, :, kvh, o, :],
                       wsrc(w_kv, kvh * 64, 64), 4096) for kvh in range(4) for o in range(2)])
        for half in range(2):
            wdef(f"q{half}", simple(w_q, half * 512, 512, 8))
        for half in range(2):
            wdef(f"o{half}", simple(w_o, half * 512, 512, 8))
        NT = len(WT)
        wsc = nc.dram_tensor("wsc", [NT, 128, 4096], BF16, kind="Internal").ap()
        stf = Ring([h.re("p c n -> p (c n)"), big32[:, 0:4096]])
        stb = Ring([xn.re("p c n -> p (c n)"), og.re("p c n -> p (c n)"), qall.re("p c n -> p (c n)")])
        cast_eng = ["dve", "act", "pool"]
        wlist = list(WT.items())
        f32s = {}

        def pre_in(i):
            name, (ti, parts, n_used) = wlist[i]
            f32t = stf.next()
            f32s[i] = f32t
            P.dma("sp", [(dfn(f32t.ap), src) for (dfn, src, _) in parts], f32t.keys[0] + "w", writes=f32t.keys)
        for i in range(min(2, len(wlist))):
            pre_in(i)
        for i, (name, (ti, parts, n_used)) in enumerate(wlist):
            f32t = f32s.pop(i)
            b16t = stb.next()
            P.copy(cast_eng[ti % 3], b16t[:, 0:n_used], f32t[:, 0:n_used])
            if i + 2 < len(wlist):
                pre_in(i + 2)
            P.dma("act", [(wsc[ti][:, 0:n_used], b16t.ap[:, 0:n_used])], b16t.keys[0] + "w", reads=b16t.keys,
                  writes=(f"wsc{ti}",))
        P.op("dve", lambda e: e.memset(kc.ap[:, 7:8], 0.0), reads=(),
             writes=tuple(["qall", "big", "kcx"] + [f"g_{n}_{i}" for i in range(2) for n in
                           ["D", "dL", "E", "Z", "Z2", "LmP", "UmP", "L3", "U3", "PLa", "PLb"]]))
        try:
            if cfg.get("stop") == 0:
                raise StopBuild()
            for seq in range(NSEQ):
                ng = TPS // 4
                for g in range(ng):
                    group("p", seq, g, 4, g == 0, g == ng - 1)
            if SAMPLE:
                group("s", 0, 0, 1, True, True)
        except StopBuild:
            pass
        P.emit()
        cfg["nops"] = P.nops
    return nc, (cpack, smask_np)


_CACHE = {}


def host_inputs(inp, core, cfg, cpack):
    NSEQ, TPS = cfg["nseq"], cfg["tps"]
    L = TPS * 128
    f = np.float32
    m = {}
    s0 = core * NSEQ
    if NSEQ:
        m["x_p"] = np.ascontiguousarray(inp["x_prompt"][s0:s0 + NSEQ, :L].reshape(NSEQ * L, D), f)
        m["pe_p"] = np.ascontiguousarray(inp["p_prompt"][:, s0:s0 + NSEQ, :L].reshape(2, NSEQ * L, 256), f)
    else:
        m["x_p"] = np.zeros((1, D), f)
        m["pe_p"] = np.zeros((2, 1, 256), f)
    b0 = core * 16
    m["x_s"] = np.ascontiguousarray(inp["x_sample"][b0:b0 + 16].reshape(128, D), f)
    m["pe_s"] = np.ascontiguousarray(inp["p_sample"][:, b0:b0 + 16].reshape(2, 128, 256), f)
    m["sd_in"] = np.ascontiguousarray(inp["state_delta"][0, b0:b0 + 16], f)
    m["sc_in"] = np.ascontiguousarray(inp["state_conv"][0, b0:b0 + 16].reshape(48, 3072), f)
    m["ck_in"] = np.ascontiguousarray(inp["cache_k_win"][b0:b0 + 16].reshape(16, 128, 256), f)
    m["cv_in"] = np.ascontiguousarray(inp["cache_v_win"][b0:b0 + 16].reshape(16, 128, 256), f)
    return m


def shared_inputs(inp, cpack):
    f = np.float32
    m = {}
    m["w_in"] = np.ascontiguousarray(inp["a_w_in"][0], f)
    m["w_out"] = np.ascontiguousarray(inp["a_w_out"][0], f)
    m["w_gu"] = np.ascontiguousarray(inp["ffn_w_gu"], f)
    m["w_down"] = np.ascontiguousarray(inp["ffn_w_down"], f)
    m["w_pp"] = np.ascontiguousarray(inp["ple_w_proj"], f)
    m["w_pg"] = np.ascontiguousarray(inp["ple_w_gate"], f)
    m["w_kv"] = np.ascontiguousarray(inp["kv_w"], f)
    m["w_q"] = np.ascontiguousarray(inp["b_w_q"][0], f)
    m["w_o"] = np.ascontiguousarray(inp["b_w_o"][0], f)
    gl = [inp["a_norm"][0], inp["ffn_norm"][0], inp["ffn_norm"][1], inp["ple_norm"][0], inp["ple_norm"][1],
          inp["kv_norm"], inp["b_norm"][0], inp["final_norm"]]
    pa = np.concatenate([np.asarray(g, f).reshape(8, 128) for g in gl] + [np.asarray(inp["a_out_norm"][0], f).reshape(1, 128)], 0)
    m["packA"] = np.ascontiguousarray(pa, f)
    m["packB"] = np.ascontiguousarray(np.asarray(inp["a_conv_w"][0], f).reshape(4 * 24, 128), f)
    m["vec8"] = np.ascontiguousarray(np.concatenate([np.asarray(inp["a_a_log"][0], f), np.asarray(inp["a_dt_bias"][0], f),
                                                     np.asarray(inp["b_sinks"][0], f)]).reshape(1, 32), f)
    m["cst"] = cpack[0]
    m["smask"] = cpack[1]
    return m


def run(inp, cfg, ncores):
    key = (cfg["nseq"], cfg["tps"], cfg["sample"])
    if key not in _CACHE:
        _CACHE[key] = build(cfg)
    nc, cpack = _CACHE[key]
    shared = shared_inputs(inp, cpack)
    in_maps = []
    for c in range(ncores):
        m = dict(shared)
        m.update(host_inputs(inp, c, cfg, cpack))
        in_maps.append(m)
    res = run_bass_kernel_spmd(nc, in_maps, core_ids=list(range(ncores)))
    return res.results


def kernel(**inputs):
    inp = {k: np.asarray(v) for k, v in inputs.items()}
    cfg = {"nseq": 2, "tps": 16, "sample": True}
    r = run(inp, cfg, 8)
    f = np.float32
    y_p = np.concatenate([x["y_p"].reshape(2, SEQ, D) for x in r], 0).astype(f)
    y_s = np.concatenate([x["y_s"].reshape(16, 8, D) for x in r], 0).astype(f)
    sd_p = np.concatenate([x["sd_p"] for x in r], 0)[None].astype(f)
    sd_s = np.concatenate([x["sd_s"] for x in r], 0)[None].astype(f)
    sc_p = np.concatenate([x["sc_p"].reshape(2, 3, 3072) for x in r], 0)[None].astype(f)
    sc_s = np.concatenate([x["sc_s"].reshape(16, 3, 3072) for x in r], 0)[None].astype(f)
    kw_p = np.concatenate([x["kw_p"].reshape(2, 128, 4, 64) for x in r], 0).astype(f)
    kw_s = np.concatenate([x["kw_s"].reshape(16, 128, 4, 64) for x in r], 0).astype(f)
    vw_p = np.concatenate([x["vw_p"].reshape(2, 128, 4, 64) for x in r], 0).astype(f)
    vw_s = np.concatenate([x["vw_s"].reshape(16, 128, 4, 64) for x in r], 0).astype(f)
    return (y_p, y_s, sd_p, sd_s, sc_p, sc_s, kw_p, kw_s, vw_p, vw_s)
```
